# Optimizing a Trainium2 kernel written in Bass

```python
import math
import jax
import jax.numpy as jnp
from jax import lax
import numpy as np

D_MODEL = 1024
BATCH = 4
SEQ = 8192
DEPTH = 4

GRID_W = 64
CTX_LEN = 256
HEAD_DIM = 64
EPS = 1e-6
NEG = -1e30
N_MOD = 6
SCALE = HEAD_DIM ** -0.5
ROPE_HALF = HEAD_DIM // 2
ROPE_AXIS_FREQS = ROPE_HALF // 2
ROPE_THETA = 10000.0
SWA_Q_HEADS = 8
SWA_KV_HEADS = 2
SWA_GROUP = SWA_Q_HEADS // SWA_KV_HEADS
WINDOW = 128
BLOCK = 128
SWA_WIDTH = SWA_Q_HEADS * HEAD_DIM
SWA_KV_WIDTH = SWA_KV_HEADS * HEAD_DIM
DIFF_HEADS = 4
DIFF_V_DIM = 2 * HEAD_DIM
DIFF_QK_WIDTH = DIFF_HEADS * 2 * HEAD_DIM
DIFF_WIDTH = DIFF_HEADS * DIFF_V_DIM
ATTN_IN = SWA_WIDTH + 2 * SWA_KV_WIDTH + 2 * DIFF_QK_WIDTH + DIFF_WIDTH
POOL_WINDOWS = (2, 4, 8, 16)
POOL_GROUPS = 4
POOL_WIDTH = D_MODEL // 2
POOL_GDIM = POOL_WIDTH // POOL_GROUPS
LRU_WIDTH = D_MODEL // 2
LRU_BLOCKS = 8
LRU_BDIM = LRU_WIDTH // LRU_BLOCKS
CONV_W = 4
CONV_LEFT = CONV_W // 2
LRU_C = 8.0
REC_IN = POOL_WIDTH + 2 * LRU_WIDTH
MIX_WIDTH = SWA_WIDTH + DIFF_WIDTH
PEER_HEADS = 8
PEER_NKEYS = 128
PEER_EXPERTS = PEER_NKEYS * PEER_NKEYS
PEER_QDIM = 256
PEER_HALF = PEER_QDIM // 2
PEER_TOPK = 16
PEER_CHUNK = 128
N_EVEN = (DEPTH + 1) // 2
N_ODD = DEPTH // 2

kernel_name = 'hybrid_prefix_diffusion_trunk'


def rmsnorm(x, w):
    xf = x.astype(jnp.float32)
    y = xf * lax.rsqrt(jnp.mean(xf * xf, axis=-1, keepdims=True) + EPS)
    return (y * w.astype(jnp.float32)).astype(x.dtype)


def modulate(h, shift, scale):
    return h * (1 + scale) + shift


def lambda_init(layer):
    return 0.8 - 0.6 * math.exp(-0.3 * layer)


def axial_rope(rows):
    row = jnp.repeat(jnp.arange(rows), GRID_W).astype(jnp.float32)
    col = jnp.tile(jnp.arange(GRID_W), rows).astype(jnp.float32)
    inv = ROPE_THETA ** (-jnp.arange(ROPE_AXIS_FREQS, dtype=jnp.float32) / ROPE_AXIS_FREQS)
    ang = jnp.concatenate([row[:, None] * inv, col[:, None] * inv], axis=-1)
    return jnp.cos(ang), jnp.sin(ang)


def apply_rope(x, cos, sin):
    shape = (1, x.shape[1]) + (1,) * (x.ndim - 3) + (ROPE_HALF,)
    c = cos.reshape(shape)
    s = sin.reshape(shape)
    xf = x.astype(jnp.float32)
    x1, x2 = xf[..., :ROPE_HALF], xf[..., ROPE_HALF:]
    return jnp.concatenate([x1 * c - x2 * s, x2 * c + x1 * s], axis=-1).astype(x.dtype)


def softmax_with_sink(logits, sink):
    sk = jnp.broadcast_to(sink.astype(jnp.float32).reshape(SWA_KV_HEADS, SWA_GROUP, 1, 1),
                          logits.shape[:-1] + (1,))
    p = jax.nn.softmax(jnp.concatenate([logits, sk], axis=-1), axis=-1)
    return p[..., :-1]


def swa_latent(q, k, v, kc, vc, sink):
    B, S = q.shape[0], q.shape[1]
    nb = S // BLOCK
    qb = q.reshape(B, nb, BLOCK, SWA_KV_HEADS, SWA_GROUP, HEAD_DIM)
    pad = ((0, 0), (BLOCK, BLOCK), (0, 0), (0, 0))
    kp = jnp.pad(k, pad).reshape(B, nb + 2, BLOCK, SWA_KV_HEADS, HEAD_DIM)
    vp = jnp.pad(v, pad).reshape(B, nb + 2, BLOCK, SWA_KV_HEADS, HEAD_DIM)
    kband = jnp.concatenate([kp[:, :-2], kp[:, 1:-1], kp[:, 2:]], axis=2)
    vband = jnp.concatenate([vp[:, :-2], vp[:, 1:-1], vp[:, 2:]], axis=2)
    s_loc = jnp.einsum('bnqhgd,bnkhd->bnhgqk', qb, kband).astype(jnp.float32) * SCALE
    s_ctx = jnp.einsum('bnqhgd,bchd->bnhgqc', qb, kc).astype(jnp.float32) * SCALE
    blk = jnp.arange(nb)[:, None]
    qpos = blk * BLOCK + jnp.arange(BLOCK)[None]
    kpos = blk * BLOCK - BLOCK + jnp.arange(3 * BLOCK)[None]
    kp3 = kpos[:, None, :]
    valid = (kp3 >= 0) & (kp3 < S) & (jnp.abs(qpos[:, :, None] - kp3) <= WINDOW)
    s_loc = jnp.where(valid[None, :, None, None], s_loc, NEG)
    p = softmax_with_sink(jnp.concatenate([s_loc, s_ctx], axis=-1), sink)
    kb = 3 * BLOCK
    o = (jnp.einsum('bnhgqk,bnkhd->bnqhgd', p[..., :kb].astype(v.dtype), vband)
         + jnp.einsum('bnhgqc,bchd->bnqhgd', p[..., kb:].astype(vc.dtype), vc))
    return o.reshape(B, S, SWA_WIDTH)


def swa_context(qc, kc, vc, sink):
    B, L = qc.shape[0], qc.shape[1]
    q5 = qc.reshape(B, L, SWA_KV_HEADS, SWA_GROUP, HEAD_DIM)
    s = jnp.einsum('bqhgd,bkhd->bhgqk', q5, kc).astype(jnp.float32) * SCALE
    p = softmax_with_sink(s, sink)
    o = jnp.einsum('bhgqk,bkhd->bqhgd', p.astype(vc.dtype), vc)
    return o.reshape(B, L, SWA_WIDTH)


def diff_core(q, k, v, lam):
    s = jnp.einsum('bqhmd,bkhmd->bhmqk', q, k).astype(jnp.float32) * SCALE
    p = jax.nn.softmax(s, axis=-1)
    w = p[:, :, 0] - lam * p[:, :, 1]
    return jnp.einsum('bhqk,bkhe->bqhe', w.astype(v.dtype), v)


def diff_post(o, subln_w, lam_init):
    B, T = o.shape[0], o.shape[1]
    return (rmsnorm(o, subln_w) * (1 - lam_init)).reshape(B, T, DIFF_WIDTH)


def diff_latent(q, k_all, v_all, lam):
    B, S = q.shape[0], q.shape[1]
    nb = S // BLOCK
    qb = q.reshape(B, nb, BLOCK, DIFF_HEADS, 2, HEAD_DIM).transpose(1, 0, 2, 3, 4, 5)
    o = lax.map(lambda qblk: diff_core(qblk, k_all, v_all, lam), qb)
    return o.transpose(1, 0, 2, 3, 4).reshape(B, S, DIFF_HEADS, DIFF_V_DIM)


def split_attn(p):
    B, T = p.shape[0], p.shape[1]
    o1 = SWA_WIDTH
    o2 = o1 + SWA_KV_WIDTH
    o3 = o2 + SWA_KV_WIDTH
    o4 = o3 + DIFF_QK_WIDTH
    o5 = o4 + DIFF_QK_WIDTH
    q, k, v, dq, dk, dv = jnp.split(p, [o1, o2, o3, o4, o5], axis=-1)
    return (q.reshape(B, T, SWA_Q_HEADS, HEAD_DIM),
            k.reshape(B, T, SWA_KV_HEADS, HEAD_DIM),
            v.reshape(B, T, SWA_KV_HEADS, HEAD_DIM),
            dq.reshape(B, T, DIFF_HEADS, 2, HEAD_DIM),
            dk.reshape(B, T, DIFF_HEADS, 2, HEAD_DIM),
            dv.reshape(B, T, DIFF_HEADS, DIFF_V_DIM))


def attn_mixer(hl, hc, w_in, w_out, sink, lam_vecs, subln_w, lam_init, cos, sin, ctx_out):
    ql, kl, vl, dql, dkl, dvl = split_attn(hl @ w_in)
    qc, kc, vc, dqc, dkc, dvc = split_attn(hc @ w_in)
    ql, kl = apply_rope(ql, cos, sin), apply_rope(kl, cos, sin)
    dql, dkl = apply_rope(dql, cos, sin), apply_rope(dkl, cos, sin)
    lv = lam_vecs.astype(jnp.float32)
    lam = jnp.exp(jnp.sum(lv[0] * lv[1])) - jnp.exp(jnp.sum(lv[2] * lv[3])) + lam_init
    a_l = swa_latent(ql, kl, vl, kc, vc, sink)
    k_all = jnp.concatenate([dkc, dkl], axis=1)
    v_all = jnp.concatenate([dvc, dvl], axis=1)
    b_l = diff_post(diff_latent(dql, k_all, v_all, lam), subln_w, lam_init)
    yl = jnp.concatenate([a_l, b_l], axis=-1) @ w_out
    yc = None
    if ctx_out:
        a_c = swa_context(qc, kc, vc, sink)
        b_c = diff_post(diff_core(dqc, dkc, dvc, lam), subln_w, lam_init)
        yc = jnp.concatenate([a_c, b_c], axis=-1) @ w_out
    return yl, yc


def pool_mix(x, pool_w, pool_scale):
    B, T = x.shape[0], x.shape[1]
    xf = x.astype(jnp.float32)
    cs = jnp.pad(jnp.cumsum(xf, axis=1), ((0, 0), (1, 0), (0, 0)))
    t = jnp.arange(T)
    outs = []
    for g, w in enumerate(POOL_WINDOWS):
        lo = jnp.clip(t - w // 2, 0, T)
        hi = jnp.clip(t - w // 2 + w, 0, T)
        csg = cs[..., g * POOL_GDIM:(g + 1) * POOL_GDIM]
        cnt = (hi - lo).astype(jnp.float32)[None, :, None]
        outs.append((csg[:, hi] - csg[:, lo]) / cnt - xf[..., g * POOL_GDIM:(g + 1) * POOL_GDIM])
    d = jnp.stack(outs, axis=2)
    y = jnp.einsum('btgd,gde->btge', d, pool_w.astype(jnp.float32)).reshape(B, T, POOL_WIDTH)
    return (y * pool_scale.astype(jnp.float32)).astype(x.dtype)


def centred_dwconv(x, w, b):
    T = x.shape[1]
    xp = jnp.pad(x, ((0, 0), (CONV_LEFT, CONV_W - 1 - CONV_LEFT), (0, 0)))
    y = b
    for k in range(CONV_W):
        y = y + w[k] * xp[:, k:k + T]
    return y


def rglru_coeffs(u, wa, ba, wx, bx, lam):
    B, T = u.shape[0], u.shape[1]
    uf = u.astype(jnp.float32)
    ub = uf.reshape(B, T, LRU_BLOCKS, LRU_BDIM)
    r = jax.nn.sigmoid(jnp.einsum('btnd,nde->btne', ub, wa.astype(jnp.float32)).reshape(B, T, LRU_WIDTH) + ba)
    i = jax.nn.sigmoid(jnp.einsum('btnd,nde->btne', ub, wx.astype(jnp.float32)).reshape(B, T, LRU_WIDTH) + bx)
    log_a = -LRU_C * r * jax.nn.softplus(-lam.astype(jnp.float32))
    a = jnp.exp(log_a)
    mult = jnp.sqrt(-jnp.expm1(2.0 * log_a))
    return a, mult * (i * uf)


def linear_scan(a, b, h0, reverse):
    if h0 is not None:
        idx = -1 if reverse else 0
        b = b.at[:, idx].add(a[:, idx] * h0)

    def comb(l, r):
        return l[0] * r[0], r[0] * l[1] + r[1]

    _, h = lax.associative_scan(comb, (a, b), reverse=reverse, axis=1)
    return h


def rec_mixer(hl, hc, w_in, w_out, pool_w, pool_scale, conv_w, conv_b, wa, ba, wx, bx, lam, ctx_out):
    splits = [POOL_WIDTH, POOL_WIDTH + LRU_WIDTH]
    xpl, xrl, gl = jnp.split(hl @ w_in, splits, axis=-1)
    xpc, xrc, gc = jnp.split(hc @ w_in, splits, axis=-1)
    ul = centred_dwconv(xrl, conv_w, conv_b)
    uc = centred_dwconv(xrc, conv_w, conv_b)
    h_lat, h_ctx = [], []
    for d, rev in enumerate((False, True)):
        ac, bc = rglru_coeffs(uc, wa[d], ba[d], wx[d], bx[d], lam[d])
        hcd = linear_scan(ac, bc, None, rev)
        h_end = hcd[:, 0] if rev else hcd[:, -1]
        al, bl = rglru_coeffs(ul, wa[d], ba[d], wx[d], bx[d], lam[d])
        h_lat.append(linear_scan(al, bl, h_end, rev))
        h_ctx.append(hcd)
    rec_l = (h_lat[0] + h_lat[1]).astype(hl.dtype) * jax.nn.gelu(gl, approximate=False)
    yl = jnp.concatenate([pool_mix(xpl, pool_w, pool_scale), rec_l], axis=-1) @ w_out
    yc = None
    if ctx_out:
        rec_c = (h_ctx[0] + h_ctx[1]).astype(hc.dtype) * jax.nn.gelu(gc, approximate=False)
        yc = jnp.concatenate([pool_mix(xpc, pool_w, pool_scale), rec_c], axis=-1) @ w_out
    return yl, yc


def peer_ffn(h, wq, subkeys, u, v):
    B, T, D = h.shape
    tok = h.reshape(-1, PEER_CHUNK, D)

    def chunk(xc):
        q = (xc @ wq).astype(jnp.float32).reshape(PEER_CHUNK, PEER_HEADS, 2, PEER_HALF)
        s = jnp.einsum('thpd,hpkd->thpk', q, subkeys.astype(jnp.float32))
        s1, i1 = lax.top_k(s[:, :, 0], PEER_TOPK)
        s2, i2 = lax.top_k(s[:, :, 1], PEER_TOPK)
        cand = (s1[..., :, None] + s2[..., None, :]).reshape(PEER_CHUNK, PEER_HEADS, PEER_TOPK * PEER_TOPK)
        cidx = (i1[..., :, None] * PEER_NKEYS + i2[..., None, :]).reshape(PEER_CHUNK, PEER_HEADS, PEER_TOPK * PEER_TOPK)
        top_s, pos = lax.top_k(cand, PEER_TOPK)
        eidx = jnp.take_along_axis(cidx, pos, axis=-1)
        g = jax.nn.softmax(top_s, axis=-1)
        act = jax.nn.gelu(jnp.einsum('td,thkd->thk', xc, u[eidx]).astype(jnp.float32), approximate=False)
        coef = (g * act).astype(xc.dtype)
        return jnp.einsum('thk,thkd->td', coef, v[eidx])

    return lax.map(chunk, tok).reshape(B, T, D)


def setup_inputs(seed: int = 0) -> dict:
    key = jax.random.key(seed)
    ks = jax.random.split(key, 32)
    f32 = jnp.float32

    def nrm(k, shape, std):
        return jax.random.normal(k, shape, f32) * std

    lru_u = jax.random.uniform(ks[22], (N_ODD, 2, LRU_WIDTH), f32, 0.9, 0.999)
    lru_p = lru_u ** (1.0 / LRU_C)
    return {
        'x': nrm(ks[0], (BATCH, SEQ, D_MODEL), 1.0),
        'c': nrm(ks[1], (BATCH, D_MODEL), 1.0),
        'ctx': nrm(ks[2], (BATCH, CTX_LEN, D_MODEL), 1.0),
        'c_ctx': nrm(ks[3], (D_MODEL,), 1.0),
        'w_mod': nrm(ks[4], (DEPTH, D_MODEL, N_MOD * D_MODEL), 0.5 * D_MODEL ** -0.5),
        'b_mod': nrm(ks[5], (DEPTH, N_MOD * D_MODEL), 0.02),
        'norm_mix': 1.0 + nrm(ks[6], (DEPTH, D_MODEL), 0.05),
        'norm_ffn': 1.0 + nrm(ks[7], (DEPTH, D_MODEL), 0.05),
        'w_out': nrm(ks[8], (DEPTH, MIX_WIDTH, D_MODEL), MIX_WIDTH ** -0.5),
        'attn_w_in': nrm(ks[9], (N_EVEN, D_MODEL, ATTN_IN), D_MODEL ** -0.5),
        'swa_sink': nrm(ks[10], (N_EVEN, SWA_Q_HEADS), 0.5),
        'diff_lambda': nrm(ks[11], (N_EVEN, 4, HEAD_DIM), 0.1),
        'diff_subln': 1.0 + nrm(ks[12], (N_EVEN, DIFF_V_DIM), 0.05),
        'rec_w_in': nrm(ks[13], (N_ODD, D_MODEL, REC_IN), D_MODEL ** -0.5),
        'pool_w': nrm(ks[14], (N_ODD, POOL_GROUPS, POOL_GDIM, POOL_GDIM), POOL_GDIM ** -0.5),
        'pool_scale': 1.0 + nrm(ks[15], (N_ODD, POOL_WIDTH), 0.05),
        'lru_conv_w': nrm(ks[16], (N_ODD, CONV_W, LRU_WIDTH), CONV_W ** -0.5),
        'lru_conv_b': nrm(ks[17], (N_ODD, LRU_WIDTH), 0.02),
        'lru_wa': nrm(ks[18], (N_ODD, 2, LRU_BLOCKS, LRU_BDIM, LRU_BDIM), LRU_BDIM ** -0.5),
        'lru_ba': nrm(ks[19], (N_ODD, 2, LRU_WIDTH), 0.02),
        'lru_wx': nrm(ks[20], (N_ODD, 2, LRU_BLOCKS, LRU_BDIM, LRU_BDIM), LRU_BDIM ** -0.5),
        'lru_bx': nrm(ks[21], (N_ODD, 2, LRU_WIDTH), 0.02),
        'lru_lambda': jnp.log(lru_p) - jnp.log1p(-lru_p),
        'peer_wq': nrm(ks[23], (DEPTH, D_MODEL, PEER_HEADS * PEER_QDIM), D_MODEL ** -0.5),
        'peer_subkeys': nrm(ks[24], (DEPTH, PEER_HEADS, 2, PEER_NKEYS, PEER_HALF), PEER_HALF ** -0.5),
        'peer_u': nrm(ks[25], (DEPTH, PEER_EXPERTS, D_MODEL), D_MODEL ** -0.5),
        'peer_v': nrm(ks[26], (DEPTH, PEER_EXPERTS, D_MODEL), PEER_HEADS ** -0.5),
        'final_norm': 1.0 + nrm(ks[27], (D_MODEL,), 0.05),
    }


def reference(x, c, ctx, c_ctx, w_mod, b_mod, norm_mix, norm_ffn, w_out, attn_w_in, swa_sink,
              diff_lambda, diff_subln, rec_w_in, pool_w, pool_scale, lru_conv_w, lru_conv_b,
              lru_wa, lru_ba, lru_wx, lru_bx, lru_lambda, peer_wq, peer_subkeys, peer_u, peer_v,
              final_norm):
    B, S, _ = x.shape
    rows = S // GRID_W
    cos, sin = axial_rope(rows)
    sc = jax.nn.silu(c)
    scc = jax.nn.silu(c_ctx)
    xl, xc = x, ctx
    for l in range(DEPTH):
        j = l // 2
        ctx_out = l < DEPTH - 1
        ml = (sc @ w_mod[l] + b_mod[l]).reshape(B, 1, N_MOD, D_MODEL)
        mc = (scc @ w_mod[l] + b_mod[l]).reshape(1, 1, N_MOD, D_MODEL)
        hl = modulate(rmsnorm(xl, norm_mix[l]), ml[:, :, 0], ml[:, :, 1])
        hc = modulate(rmsnorm(xc, norm_mix[l]), mc[:, :, 0], mc[:, :, 1])
        if l % 2 == 0:
            yl, yc = attn_mixer(hl, hc, attn_w_in[j], w_out[l], swa_sink[j], diff_lambda[j],
                                diff_subln[j], lambda_init(l), cos, sin, ctx_out)
        else:
            yl, yc = rec_mixer(hl, hc, rec_w_in[j], w_out[l], pool_w[j], pool_scale[j],
                               lru_conv_w[j], lru_conv_b[j], lru_wa[j], lru_ba[j], lru_wx[j],
                               lru_bx[j], lru_lambda[j], ctx_out)
        xl = xl + ml[:, :, 2] * yl
        hl = modulate(rmsnorm(xl, norm_ffn[l]), ml[:, :, 3], ml[:, :, 4])
        xl = xl + ml[:, :, 5] * peer_ffn(hl, peer_wq[l], peer_subkeys[l], peer_u[l], peer_v[l])
        if ctx_out:
            xc = xc + mc[:, :, 2] * yc
            hc = modulate(rmsnorm(xc, norm_ffn[l]), mc[:, :, 3], mc[:, :, 4])
            xc = xc + mc[:, :, 5] * peer_ffn(hc, peer_wq[l], peer_subkeys[l], peer_u[l], peer_v[l])
    return rmsnorm(xl, final_norm)
```

```python
import math
from contextlib import ExitStack

import numpy as np
import ml_dtypes
import concourse.bass as bass
import concourse.mybir as mybir
from concourse.bass_utils import run_bass_kernel_spmd

F32 = mybir.dt.float32
BF16 = mybir.dt.bfloat16
U32 = mybir.dt.uint32
I32 = mybir.dt.int32
AF = mybir.ActivationFunctionType
ALU = mybir.AluOpType
AX = mybir.AxisListType

ENGS = ("pe", "act", "dve", "pool", "sp")
NCORES = 8
D = 1024
EPS = 1e-6
NEG = -1.0e30


class Res:
    __slots__ = ("name", "writes", "reads", "dsem")

    def __init__(self, name):
        self.name = name
        self.writes = {}
        self.reads = {}
        self.dsem = None


class T:
    __slots__ = ("t", "r")

    def __init__(self, t, name):
        self.t = t
        self.r = Res(name)

    def __getitem__(self, k):
        return self.t[k]


class Sched:
    _uid = [0]

    def __init__(self, nc, es):
        self.nc = nc
        self.es = es
        Sched._uid[0] += 1
        self.uid = Sched._uid[0]
        self.streams = {e: [] for e in ENGS}
        self.semh = {}
        self.semtot = {}
        self.known = {e: {} for e in ENGS}
        for e in ENGS:
            self._newsem("E_" + e)
        self.ndsem = 0

    def _newsem(self, key):
        self.semh[key] = self.nc.alloc_semaphore(name="s%d_%s" % (self.uid, key))
        self.semtot[key] = 0
        return key

    def _collect(self, eng, reads, writes):
        need = {}

        def add(d):
            for k, v in d.items():
                if v > need.get(k, 0):
                    need[k] = v
        for r in reads:
            add(r.writes)
        for w in writes:
            add(w.writes)
            add(w.reads)
        out = []
        kn = self.known[eng]
        for k, v in need.items():
            if k == "E_pe" and eng == "pe":
                continue
            if k.startswith("D_"):
                v = self.semtot[k]
            if kn.get(k, 0) >= v:
                continue
            kn[k] = v
            out.append((k, v))
        return out

    def op(self, eng, fn, R=(), W=()):
        R = [x.r if isinstance(x, T) else x for x in R]
        W = [x.r if isinstance(x, T) else x for x in W]
        waits = self._collect(eng, R, W)
        key = "E_" + eng
        self.semtot[key] += 1
        seq = self.semtot[key]
        self.streams[eng].append((waits, fn, key, 1))
        for w in W:
            w.writes = {key: seq}
            w.reads = {}
        for r in R:
            r.reads[key] = seq
        return seq

    def dma(self, out, in_, R=(), W=(), queue="sp", **kw):
        R = [x.r if isinstance(x, T) else x for x in R]
        W = [x.r if isinstance(x, T) else x for x in W]
        waits = self._collect(queue, R, W)
        sr = W[0] if W else R[0]
        if sr.dsem is None:
            self.ndsem += 1
            sr.dsem = self._newsem("D_%d_%s" % (self.ndsem, sr.name))
        key = sr.dsem
        self.semtot[key] += 16
        val = self.semtot[key]
        self.streams[queue].append((waits, (lambda e: e.dma_start(out=out, in_=in_, **kw)), key, 16))
        for w in W:
            w.writes = {key: val}
            w.reads = {}
        for r in R:
            r.reads[key] = val

    def coll(self, kind, op, groups, src_ap, dst_ap, R=(), W=()):
        R = [x.r if isinstance(x, T) else x for x in R]
        W = [x.r if isinstance(x, T) else x for x in W]
        waits = self._collect("pool", R, W)
        self.ndsem += 1
        key = self._newsem("D_c%d" % self.ndsem)
        self.semtot[key] = 1
        self.streams["pool"].append((waits, (lambda e: e.collective_compute(
            kind, op, replica_groups=groups, ins=[src_ap.opt()], outs=[dst_ap.opt()])), key, None))
        for w in W:
            w.writes = {key: 1}
            w.reads = {}
        for r in R:
            r.reads[key] = 1

    def finish(self, final_res=None, eng="sp"):
        for e in ENGS:
            waits = [(k, v) for k, v in self.semtot.items() if v > 0 and self.known[e].get(k, 0) < v]
            self.streams[e].append((waits, None, None, 0))

    def emit(self):
        nc = self.nc
        engobj = {"pe": "tensor", "act": "scalar", "dve": "vector", "pool": "gpsimd", "sp": "sync"}
        with nc.Block() as block:
            for e in ENGS:
                stream = self.streams[e]

                def body(engine, stream=stream):
                    for waits, fn, key, inc in stream:
                        for (k, v) in waits:
                            engine.wait_ge(self.semh[k], v)
                        if fn is not None:
                            if inc is None:
                                fn(engine).then_inc(self.semh[key])
                            else:
                                fn(engine).then_inc(self.semh[key], inc)
                getattr(block, engobj[e])(body)


class Ctx:
    def __init__(self):
        self.nc = bass.Bass("TRN2", target_bir_lowering=False)
        self.dram = {}
        self.stage_id = 0
        self.cm = None

    def begin(self, dram):
        self.stage_id += 1
        self.dram = dict(dram)
        self.cm = self.nc.cleanup_on_exit()
        self.cm.__enter__()
        self.es = ExitStack()
        self.S = Sched(self.nc, self.es)

    def end(self):
        self.S.finish()
        self.S.emit()
        self.nc.all_engine_barrier()
        self.es.pop_all()
        self.cm.__exit__(None, None, None)
        self.cm = None

    def _nm(self, name):
        return "g%d_%s" % (self.stage_id, name)

    def ext_in(self, name, shape, dt=F32):
        return self.nc.dram_tensor(name, list(shape), dt, kind="ExternalInput").ap()

    def ext_out(self, name, shape, dt=F32):
        return self.nc.dram_tensor(name, list(shape), dt, kind="ExternalOutput").ap()

    def scratch(self, name, shape, dt=F32):
        return self.nc.dram_tensor(name, list(shape), dt, kind="Internal").ap()

    def din(self, name, shape=None, dt=F32):
        ap = self.dram[name]
        if shape is not None:
            assert list(ap.shape) == list(shape), (name, ap.shape, shape)
        return T(ap, name)

    dout = din

    def sb(self, name, shape, dt=F32):
        return T(self.es.enter_context(self.nc.sbuf_tensor(self._nm(name), list(shape), dt)), name)

    def ps(self, name, shape, dt=F32):
        return T(self.es.enter_context(self.nc.psum_tensor(self._nm(name), list(shape), dt)), name)

    def view(self, base, ap):
        v = T(ap, base.r.name)
        v.r = base.r
        return v

    def close(self, outs=None):
        self.end()
        return self.nc


def emit_consts(C):
    S = C.S
    identb = C.sb("identb", [128, 128], BF16)
    identf = C.sb("identf", [128, 128], F32)
    ones1 = C.sb("ones1", [1, 128], F32)
    for idt in (identb, identf):
        S.op("pool", lambda e, idt=idt: e.memset(idt[:], 0.0), W=[idt])
        S.op("pool", lambda e, idt=idt: e.affine_select(
            out=idt[:], in_=idt[:], pattern=[[-1, 128]], compare_op=ALU.not_equal, fill=1.0,
            base=0, channel_multiplier=1), R=[idt], W=[idt])
    S.op("pool", lambda e: e.memset(ones1[:], 1.0), W=[ones1])
    C.identb, C.identf, C.ones1 = identb, identf, ones1


def emit_mod(C, c2, wmod, bmod, cols, psA, psB, wm, bm):
    S = C.S
    nj = len(cols)
    modbc = C.sb("modbc", [128, 2, nj, 1024], F32)
    cs = C.sb("cs", [128, 2, 8], F32)
    csg = C.sb("csg", [128, 2, 8], F32)
    csb = C.sb("csb", [128, 16, 128], F32)
    S.dma(cs[:], c2[:].rearrange("v (k p) -> p v k", p=128), R=[c2], W=[cs], allow_slow_non_contiguous=True)
    S.dma(bm[0:1, 0:nj * 1024], bmod[0:nj * 1024].unsqueeze(0), R=[bmod], W=[bm])
    S.op("act", lambda e: e.activation(out=csg[:], in_=cs[:], func=AF.Sigmoid), R=[cs], W=[csg])
    S.op("dve", lambda e: e.tensor_tensor(out=csg[:], in0=csg[:], in1=cs[:], op=ALU.mult), R=[csg, cs], W=[csg])
    S.op("dve", lambda e: e.tensor_copy(
        out=csb[:], in_=csg[:].rearrange("p v k -> p (v k)").unsqueeze(2).to_broadcast([128, 16, 128])),
        R=[csg], W=[csb])
    it = 0
    pss = [psA, psB]
    for jj, j in enumerate(cols):
        for half in range(2):
            c0 = jj * 1024 + half * 512
            b0 = c0
            w = wm[it % 2]
            S.dma(w[:, :, :], wmod[:, c0:c0 + 512].rearrange("(k p) n -> p k n", p=128), R=[wmod], W=[w])
            for v in range(2):
                ps = pss[(it * 2 + v) % 2]
                for k in range(8):
                    S.op("pe", lambda e, ps=ps, v=v, k=k, w=w: e.matmul(
                        ps[:, :], lhsT=csb[:, v * 8 + k, :], rhs=w[:, k, :], start=(k == 0), stop=False),
                        R=[csb, w], W=[ps])
                S.op("pe", lambda e, ps=ps, b0=b0: e.matmul(
                    ps[:, :], lhsT=C.ones1[0:1, :], rhs=bm[0:1, b0:b0 + 512], start=False, stop=True),
                    R=[C.ones1, bm], W=[ps])
                S.op("act", lambda e, ps=ps, v=v, jj=jj, half=half: e.activation(
                    out=modbc[:, v, jj, half * 512:(half + 1) * 512], in_=ps[:, :], func=AF.Copy),
                    R=[ps], W=[modbc])
            it += 1
    return modbc


def emit_rmsnorm_mod(C, xt, W1v, shv, hb, tag, scratch, ss, n=1024):
    S = C.S
    S.op("act", lambda e: e.activation(out=scratch[:, :n], in_=xt[:, :n], func=AF.Square, accum_out=ss[:, 0:1]),
         R=[xt], W=[scratch, ss])
    S.op("dve", lambda e: e.tensor_scalar(out=ss[:, 1:2], in0=ss[:, 0:1], scalar1=1.0 / n, scalar2=EPS,
                                          op0=ALU.mult, op1=ALU.add), R=[ss], W=[ss])
    S.op("act", lambda e: e.activation(out=ss[:, 2:3], in_=ss[:, 1:2], func=AF.Sqrt), R=[ss], W=[ss])
    S.op("dve", lambda e: e.reciprocal(out=ss[:, 3:4], in_=ss[:, 2:3]), R=[ss], W=[ss])
    S.op("dve", lambda e: e.scalar_tensor_tensor(out=scratch[:, :n], in0=xt[:, :n], scalar=ss[:, 3:4], in1=W1v,
                                                 op0=ALU.mult, op1=ALU.mult), R=[xt, ss, C.modres], W=[scratch])
    if shv is not None:
        S.op("pool", lambda e: e.tensor_tensor(out=hb[:, :n], in0=scratch[:, :n], in1=shv, op=ALU.add),
             R=[scratch, C.modres], W=[hb])
    else:
        S.op("pool", lambda e: e.tensor_copy(out=hb[:, :n], in_=scratch[:, :n]), R=[scratch], W=[hb])


NFM = 14
WALL_EVEN = NFM * 128 * 2 + 640


def build_f_even(C, NT, NCT):
    S = C.S
    TT = NT * 128
    xin = C.din("xin", [TT, D])
    c2 = C.din("c2", [2, D])
    wmod = C.din("wmod", [D, 2 * D])
    bmod = C.din("bmod", [2 * D])
    nw = C.din("nw", [D])
    wall = C.din("wall", [D, WALL_EVEN])
    cosT = C.din("cosT", [128, TT])
    sinT = C.din("sinT", [128, TT])
    qTo = C.dout("qT")
    dqTo = C.dout("dqT")
    dkTo = C.dout("dkT")
    kwTo = C.dout("kwT")
    vwo = C.dout("vw")
    dvo = C.dout("dv")

    emit_consts(C)
    psT = [C.ps("psT%d" % i, [128, 8, 128], BF16) for i in range(2)]
    psP = [C.ps("psP%d" % i, [128, 512], F32) for i in range(2)]
    psV = [C.ps("psV%d" % i, [128, 512], F32) for i in range(2)]
    psV2 = C.ps("psV2", [128, 512], F32)

    wsb = C.sb("wsb", [128, 8, WALL_EVEN], BF16)
    wv = wall[:].rearrange("(k p) n -> p k n", p=128)
    for k in range(8):
        for c0 in range(0, WALL_EVEN, 1024):
            c1 = min(c0 + 1024, WALL_EVEN)
            S.dma(wsb[:, k, c0:c1], wv[:, k, c0:c1], R=[wall], W=[wsb], queue="pool")
    cos_sbs = [C.sb("cos_sb%d" % i, [128, 256], F32) for i in range(2)]
    sin_sbs = [C.sb("sin_sb%d" % i, [128, 256], F32) for i in range(2)]
    nwb = C.sb("nwb", [128, D], F32)
    S.dma(nwb[:], nw[:].partition_broadcast(128), R=[nw], W=[nwb])

    scrA = C.sb("scrA", [128, 4096], F32)
    scrB = C.sb("scrB", [128, 4096], F32)
    scrC = C.sb("scrC", [128, 2048], F32)
    wm = [C.view(b, b[:].rearrange("p (k n) -> p k n", k=8)) for b in (scrA, scrB)]
    modbc = emit_mod(C, c2, wmod, bmod, [0, 1], psP[0], psP[1], wm, scrC)
    C.modres = modbc.r
    for v in range(2):
        S.op("dve", lambda e, v=v: e.scalar_tensor_tensor(
            out=modbc[:, v, 1, :], in0=modbc[:, v, 1, :], scalar=1.0, in1=nwb[:], op0=ALU.add, op1=ALU.mult),
            R=[modbc, nwb], W=[modbc])

    xts = [C.sb("xt%d" % i, [128, D], F32) for i in range(2)]
    scr = C.view(scrC, scrC[:, 0:D])
    hbs = [C.sb("hb%d" % i, [128, D], BF16) for i in range(2)]
    sss = [C.sb("ss%d" % i, [128, 4], F32) for i in range(2)]
    hTs = [C.sb("hT%d" % i, [128, 8, 256], BF16) for i in range(2)]
    t1s = [C.sb("t1_%d" % i, [128, 256], F32) for i in range(2)]
    t2s = [C.sb("t2_%d" % i, [128, 256], F32) for i in range(2)]
    qos = [C.view(b, b[:].bitcast(BF16)[:, 0:NFM * 256].rearrange("p (c t) -> p c t", c=NFM)) for b in (scrA, scrB)]
    vos = [C.sb("vo%d" % i, [128, 640], BF16) for i in range(2)]
    zt = C.sb("zt", [128, 256], BF16)
    S.op("pool", lambda e: e.memset(zt[:], 0.0), W=[zt])
    CTXT = NCT * 128
    for c0 in (CTXT, TT + 128):
        S.dma(kwTo[:, c0:c0 + 128].rearrange("(c p) t -> p c t", p=128),
              zt[:].rearrange("p (c t) -> p c t", c=2), R=[zt], W=[kwTo])
        S.dma(vwo[c0:c0 + 128, :], zt[:, 0:128], R=[zt], W=[vwo])

    def fmv(t_):
        return t_[:].rearrange("(c p) t -> p c t", p=128)

    nblk = NT // 2
    OFF_SW = NFM * 128
    OFF_V = 2 * NFM * 128
    for b in range(nblk):
        hT = hTs[b % 2]
        for tl in range(2):
            t = b * 2 + tl
            v = 1 if t < NCT else 0
            xt = xts[t % 2]
            hb = hbs[t % 2]
            ss = sss[t % 2]
            S.dma(xt[:], xin[t * 128:(t + 1) * 128, :], R=[xin], W=[xt])
            emit_rmsnorm_mod(C, xt, modbc[:, v, 1, :], modbc[:, v, 0, :], hb, "f", scr, ss)
            pT = psT[t % 2]
            for k in range(8):
                S.op("pe", lambda e, pT=pT, hb=hb, k=k: e.transpose(
                    out=pT[:, k, :], in_=hb[:, k * 128:(k + 1) * 128], identity=C.identb[:]),
                    R=[hb, C.identb], W=[pT])
            S.op("act", lambda e, pT=pT, hT=hT, tl=tl: e.activation(
                out=hT[:, :, tl * 128:(tl + 1) * 128], in_=pT[:, :, :], func=AF.Copy), R=[pT], W=[hT])
            pv = psV[t % 2]
            for k in range(8):
                S.op("pe", lambda e, pv=pv, hT=hT, k=k, tl=tl: e.matmul(
                    pv[:, :], lhsT=hT[:, k, tl * 128:(tl + 1) * 128], rhs=wsb[:, k, OFF_V:OFF_V + 512],
                    start=(k == 0), stop=(k == 7)), R=[hT, wsb], W=[pv])
            for k in range(8):
                S.op("pe", lambda e, hT=hT, k=k, tl=tl: e.matmul(
                    psV2[:, 0:128], lhsT=hT[:, k, tl * 128:(tl + 1) * 128], rhs=wsb[:, k, OFF_V + 512:OFF_V + 640],
                    start=(k == 0), stop=(k == 7)), R=[hT, wsb], W=[psV2])
            vo = vos[t % 2]
            S.op("act", lambda e, pv=pv, vo=vo: e.activation(out=vo[:, 0:512], in_=pv[:, :], func=AF.Copy),
                 R=[pv], W=[vo])
            S.op("act", lambda e, vo=vo: e.activation(out=vo[:, 512:640], in_=psV2[:, 0:128], func=AF.Copy),
                 R=[psV2], W=[vo])
            wr = t * 128 if t < NCT else t * 128 + 128
            S.dma(vwo[wr:wr + 128, :], vo[:, 0:128], R=[vo], W=[vwo])
            S.dma(dvo[t * 128:(t + 1) * 128, :], vo[:, 128:640], R=[vo], W=[dvo])
        qo = qos[b % 2]
        tok0 = b * 256
        cos_sb, sin_sb = cos_sbs[b % 2], sin_sbs[b % 2]
        S.dma(cos_sb[:], cosT[:, tok0:tok0 + 256], R=[cosT], W=[cos_sb])
        S.dma(sin_sb[:], sinT[:, tok0:tok0 + 256], R=[sinT], W=[sin_sb])
        for ch in range(NFM):
            pp = psP[ch % 2]
            for k in range(8):
                S.op("pe", lambda e, pp=pp, hT=hT, k=k, ch=ch: e.matmul(
                    pp[:, 0:256], lhsT=wsb[:, k, ch * 128:(ch + 1) * 128], rhs=hT[:, k, :],
                    start=(k == 0), stop=(k == 7)), R=[hT, wsb], W=[pp])
            for k in range(8):
                S.op("pe", lambda e, pp=pp, hT=hT, k=k, ch=ch: e.matmul(
                    pp[:, 256:512], lhsT=wsb[:, k, OFF_SW + ch * 128:OFF_SW + (ch + 1) * 128], rhs=hT[:, k, :],
                    start=(k == 0), stop=(k == 7)), R=[hT, wsb], W=[pp])
            t1 = t1s[ch % 2]
            t2 = t2s[ch % 2]
            S.op("dve", lambda e, pp=pp, t1=t1, cos_sb=cos_sb: e.tensor_tensor(
                out=t1[:], in0=pp[:, 0:256], in1=cos_sb[:], op=ALU.mult), R=[pp, cos_sb], W=[t1])
            S.op("dve", lambda e, pp=pp, t2=t2, sin_sb=sin_sb: e.tensor_tensor(
                out=t2[:], in0=pp[:, 256:512], in1=sin_sb[:], op=ALU.mult), R=[pp, sin_sb], W=[t2])
            S.op("pool", lambda e, t1=t1, t2=t2, qo=qo, ch=ch: e.tensor_tensor(
                out=qo[:, ch, :], in0=t1[:], in1=t2[:], op=ALU.add), R=[t1, t2], W=[qo])
        wc = tok0 if b * 2 < NCT else tok0 + 128
        S.dma(fmv(qTo)[:, :, tok0:tok0 + 256], qo[:, 0:4, :], R=[qo], W=[qTo])
        S.dma(fmv(kwTo)[:, :, wc:wc + 256], qo[:, 4:6, :], R=[qo], W=[kwTo])
        S.dma(fmv(dqTo)[:, :, tok0:tok0 + 256], qo[:, 6:10, :], R=[qo], W=[dqTo])
        S.dma(fmv(dkTo)[:, :, tok0:tok0 + 256], qo[:, 10:14, :], R=[qo], W=[dkTo])
    return


SCALE = 0.125


def build_att(C, NQT, NCT, NCK, NKC):
    S = C.S
    TQ = NQT * 128
    NLQ = NQT - NCT
    NWC = NCK + NLQ + 2
    qT = C.din("qT", [4 * 128, TQ], BF16)
    dqT = C.din("dqT", [4 * 128, TQ], BF16)
    kwT = C.din("kwT", [2 * 128, NWC * 128], BF16)
    vw = C.din("vw", [NWC * 128, 128], BF16)
    dkT = C.din("dkT", [4 * 128, NKC * 128], BF16)
    dv = C.din("dv", [NKC * 128, 512], BF16)
    masks = C.din("masks", [4, 128, 128], F32)
    sink = C.din("sink", [8])
    lamv = C.din("lamv", [4, 64])
    subln = C.din("subln", [128])
    lconst = C.din("lconst", [2])
    concat = C.dout("concat", [TQ, D])

    emit_consts(C)
    acc = [C.ps("acc%d" % i, [128, 512], F32) for i in range(4)]
    pss = [C.ps("pss%d" % i, [128, 512], F32) for i in range(4)]

    msk = C.sb("msk", [128, 4, 128], BF16)
    mskf = C.sb("mskf", [128, 4, 128], F32)
    S.dma(mskf[:], masks[:].rearrange("m p j -> p m j"), R=[masks], W=[mskf])
    S.op("dve", lambda e: e.tensor_copy(out=msk[:], in_=mskf[:]), R=[mskf], W=[msk])
    esink = C.sb("esink", [128, 8], F32)
    S.dma(esink[:], sink[:].partition_broadcast(128), R=[sink], W=[esink])
    S.op("act", lambda e: e.activation(out=esink[:], in_=esink[:], func=AF.Exp), R=[esink], W=[esink])
    lc = C.sb("lc", [128, 2], F32)
    S.dma(lc[:], lconst[:].partition_broadcast(128), R=[lconst], W=[lc])
    slw = C.sb("slw", [128, 128], F32)
    S.dma(slw[:], subln[:].partition_broadcast(128), R=[subln], W=[slw])
    S.op("dve", lambda e: e.tensor_scalar(out=slw[:], in0=slw[:], scalar1=lc[:, 1:2], scalar2=None, op0=ALU.mult),
         R=[slw, lc], W=[slw])
    lvb = C.sb("lvb", [128, 4, 64], F32)
    S.dma(lvb[:], lamv[:].rearrange("a d -> (a d)").partition_broadcast(128), R=[lamv], W=[lvb])
    lsc = C.sb("lsc", [128, 8], F32)
    lpr = C.sb("lpr", [128, 2, 64], F32)
    S.op("dve", lambda e: e.tensor_tensor(out=lpr[:, 0, :], in0=lvb[:, 0, :], in1=lvb[:, 1, :], op=ALU.mult), R=[lvb], W=[lpr])
    S.op("dve", lambda e: e.tensor_tensor(out=lpr[:, 1, :], in0=lvb[:, 2, :], in1=lvb[:, 3, :], op=ALU.mult), R=[lvb, lpr], W=[lpr])
    S.op("dve", lambda e: e.tensor_reduce(out=lsc[:, 0:2], in_=lpr[:], op=ALU.add, axis=AX.X), R=[lpr], W=[lsc])
    S.op("act", lambda e: e.activation(out=lsc[:, 2:4], in_=lsc[:, 0:2], func=AF.Exp), R=[lsc], W=[lsc])
    S.op("dve", lambda e: e.tensor_tensor(out=lsc[:, 4:5], in0=lsc[:, 3:4], in1=lsc[:, 2:3], op=ALU.subtract), R=[lsc], W=[lsc])
    S.op("dve", lambda e: e.tensor_tensor(out=lsc[:, 5:6], in0=lsc[:, 4:5], in1=lc[:, 0:1], op=ALU.subtract), R=[lsc, lc], W=[lsc])
    nlam = lsc

    pTs = [C.sb("pT%d" % i, [128, 512], BF16) for i in range(4)]
    fin = C.sb("fin", [128, 16], F32)
    npt = [0]

    def next_pt():
        i = npt[0] % 4
        npt[0] += 1
        return pss[i], pTs[i]

    kds = [C.sb("kd%d" % i, [128, NWC * 128], BF16) for i in range(2)]
    vgs = [C.sb("vg%d" % i, [128, NWC, 65], BF16) for i in range(2)]
    qqs = [C.sb("qq%d" % i, [128, 2, 128], BF16) for i in range(2)]
    oswa = [C.sb("oswa%d" % i, [128, 4, 64], F32) for i in range(2)]
    for g in range(2):
        kd, vg = kds[g], vgs[g]
        S.dma(kd[:], kwT[g * 128:(g + 1) * 128, :], R=[kwT], W=[kd])
        S.op("pool", lambda e, vg=vg: e.memset(vg[:, :, 64:65], 1.0), W=[vg])
        S.dma(vg[:, :, 0:64], vw[:, g * 64:(g + 1) * 64].rearrange("(c p) d -> p c d", p=128), R=[vw], W=[vg])
        for t in range(NQT):
            qq = qqs[t % 2]
            S.dma(qq[:], qT[2 * g * 128:(2 * g + 2) * 128, t * 128:(t + 1) * 128].rearrange("(c p) t -> p c t", p=128),
                  R=[qT], W=[qq])
            if t < NCT:
                kl = [(kc, None) for kc in range(NCK)]
            else:
                n = t - NCT
                kl = [(kc, None) for kc in range(NCK)]
                kl += [(NCK + n, 0 if n == 0 else 1), (NCK + n + 1, None), (NCK + n + 2, 3 if n == NLQ - 1 else 2)]
            for ki, (kc, mi) in enumerate(kl):
                psA, pT = next_pt()
                psB, pT2 = next_pt()
                for j in range(4):
                    cc, hp = j // 2, j % 2
                    ps = psA if hp == 0 else psB
                    S.op("pe", lambda e, ps=ps, j=j, cc=cc, hp=hp, kc=kc, kd=kd, qq=qq: e.matmul(
                        ps[:, cc * 128:(cc + 1) * 128], lhsT=kd[hp * 64:(hp + 1) * 64, kc * 128:(kc + 1) * 128],
                        rhs=qq[hp * 64:(hp + 1) * 64, cc, :], start=True, stop=True), R=[kd, qq], W=[ps])
                S.op("act", lambda e, psA=psA, pT=pT: e.activation(out=pT[:, 0:256], in_=psA[:, 0:256], func=AF.Exp, scale=SCALE),
                     R=[psA], W=[pT])
                S.op("act", lambda e, psB=psB, pT=pT: e.activation(out=pT[:, 256:512], in_=psB[:, 0:256], func=AF.Exp, scale=SCALE),
                     R=[psB], W=[pT])
                if mi is not None:
                    S.op("pool", lambda e, pT=pT, mi=mi: e.tensor_tensor(
                        out=pT[:].rearrange("p (j q) -> p j q", j=4), in0=pT[:].rearrange("p (j q) -> p j q", j=4),
                        in1=msk[:, mi, :].unsqueeze(1).to_broadcast([128, 4, 128]), op=ALU.mult),
                        R=[pT, msk], W=[pT])
                for j in range(4):
                    cc, hp = j // 2, j % 2
                    col = (hp * 2 + cc) * 128
                    S.op("pe", lambda e, j=j, pT=pT, kc=kc, vg=vg, ki=ki, nk=len(kl), col=col: e.matmul(
                        acc[j][:, 0:65], lhsT=pT[:, col:col + 128], rhs=vg[:, kc, :],
                        start=(ki == 0), stop=(ki == nk - 1)), R=[pT, vg], W=[acc[j]])
            ot = oswa[t % 2]
            for j in range(4):
                hd = 4 * g + j
                S.op("dve", lambda e, j=j, hd=hd: e.tensor_tensor(
                    out=fin[:, j:j + 1], in0=acc[j][:, 64:65], in1=esink[:, hd:hd + 1], op=ALU.add),
                    R=[acc[j], esink], W=[fin])
                S.op("dve", lambda e, j=j: e.reciprocal(out=fin[:, 4 + j:5 + j], in_=fin[:, j:j + 1]), R=[fin], W=[fin])
                S.op("dve", lambda e, j=j, ot=ot: e.tensor_scalar(
                    out=ot[:, j, :], in0=acc[j][:, 0:64], scalar1=fin[:, 4 + j:5 + j], scalar2=None, op0=ALU.mult),
                    R=[acc[j], fin], W=[ot])
            S.dma(concat[t * 128:(t + 1) * 128, g * 256:(g + 1) * 256], ot[:].rearrange("p j d -> p (j d)"),
                  R=[ot], W=[concat])

    dks = [C.sb("dk%d" % i, [128, NKC * 128], BF16) for i in range(2)]
    vhs = [C.sb("vh%d" % i, [128, NKC, 129], BF16) for i in range(2)]
    dqs = [C.sb("dq%d" % i, [128, 256], BF16) for i in range(2)]
    odf = [C.sb("odf%d" % i, [128, 128], F32) for i in range(2)]
    t128 = C.sb("t128", [128, 128], F32)
    scr128 = C.sb("scr128", [128, 128], F32)
    fd = C.sb("fd", [128, 16], F32)
    nqb = NQT // 2
    nout = 0
    for h in range(4):
        dk, vh = dks[h % 2], vhs[h % 2]
        S.dma(dk[:], dkT[h * 128:(h + 1) * 128, :], R=[dkT], W=[dk])
        S.op("pool", lambda e, vh=vh: e.memset(vh[:, :, 128:129], 1.0), W=[vh])
        S.dma(vh[:, :, 0:128], dv[:, h * 128:(h + 1) * 128].rearrange("(c p) d -> p c d", p=128), R=[dv], W=[vh])
        for qb in range(nqb):
            dq = dqs[qb % 2]
            q0 = qb * 256
            S.dma(dq[:], dqT[h * 128:(h + 1) * 128, q0:q0 + 256], R=[dqT], W=[dq])
            nk = NCK if qb * 2 < NCT else NKC
            for kc0 in range(0, nk, 2):
                psm0, pTm0 = next_pt()
                psm1, pTm1 = next_pt()
                pp = [(psm0, pTm0), (psm1, pTm1)]
                nkk = min(2, nk - kc0)
                for u in range(nkk):
                    kc = kc0 + u
                    for m in range(2):
                        ps = pp[m][0]
                        S.op("pe", lambda e, ps=ps, m=m, kc=kc, u=u, dk=dk, dq=dq: e.matmul(
                            ps[:, u * 256:(u + 1) * 256], lhsT=dk[m * 64:(m + 1) * 64, kc * 128:(kc + 1) * 128],
                            rhs=dq[m * 64:(m + 1) * 64, :], start=True, stop=True), R=[dk, dq], W=[ps])
                for m in range(2):
                    ps, pT = pp[m]
                    S.op("act", lambda e, ps=ps, pT=pT, nkk=nkk: e.activation(
                        out=pT[:, 0:nkk * 256], in_=ps[:, 0:nkk * 256], func=AF.Exp, scale=SCALE), R=[ps], W=[pT])
                for u in range(nkk):
                    kc = kc0 + u
                    for m in range(2):
                        pT = pp[m][1]
                        for sb_ in range(2):
                            a = acc[m * 2 + sb_]
                            S.op("pe", lambda e, a=a, u=u, sb_=sb_, pT=pT, kc=kc, vh=vh, nk=nk: e.matmul(
                                a[:, 0:129], lhsT=pT[:, u * 256 + sb_ * 128:u * 256 + (sb_ + 1) * 128], rhs=vh[:, kc, :],
                                start=(kc == 0), stop=(kc == nk - 1)), R=[pT, vh], W=[a])
            for sb_ in range(2):
                a0, a1 = acc[sb_], acc[2 + sb_]
                o = odf[nout % 2]
                nout += 1
                S.op("dve", lambda e, a0=a0: e.reciprocal(out=fd[:, 0:1], in_=a0[:, 128:129]), R=[a0], W=[fd])
                S.op("dve", lambda e, a1=a1: e.reciprocal(out=fd[:, 1:2], in_=a1[:, 128:129]), R=[a1, fd], W=[fd])
                S.op("dve", lambda e: e.tensor_tensor(out=fd[:, 2:3], in0=fd[:, 1:2], in1=nlam[:, 5:6], op=ALU.mult),
                     R=[fd, nlam], W=[fd])
                S.op("dve", lambda e, a0=a0: e.tensor_scalar(out=t128[:], in0=a0[:, 0:128], scalar1=fd[:, 0:1],
                                                             scalar2=None, op0=ALU.mult), R=[a0, fd], W=[t128])
                S.op("dve", lambda e, a1=a1: e.scalar_tensor_tensor(
                    out=t128[:], in0=a1[:, 0:128], scalar=fd[:, 2:3], in1=t128[:], op0=ALU.mult, op1=ALU.add),
                    R=[a1, fd, t128], W=[t128])
                S.op("act", lambda e: e.activation(out=scr128[:], in_=t128[:], func=AF.Square, accum_out=fd[:, 3:4]),
                     R=[t128], W=[scr128, fd])
                S.op("dve", lambda e: e.tensor_scalar(out=fd[:, 4:5], in0=fd[:, 3:4], scalar1=1.0 / 128, scalar2=EPS,
                                                      op0=ALU.mult, op1=ALU.add), R=[fd], W=[fd])
                S.op("act", lambda e: e.activation(out=fd[:, 5:6], in_=fd[:, 4:5], func=AF.Sqrt), R=[fd], W=[fd])
                S.op("dve", lambda e: e.reciprocal(out=fd[:, 6:7], in_=fd[:, 5:6]), R=[fd], W=[fd])
                S.op("dve", lambda e, o=o: e.scalar_tensor_tensor(
                    out=o[:], in0=t128[:], scalar=fd[:, 6:7], in1=slw[:], op0=ALU.mult, op1=ALU.mult),
                    R=[t128, fd, slw], W=[o])
                r0 = q0 + sb_ * 128
                S.dma(concat[r0:r0 + 128, 512 + h * 128:512 + (h + 1) * 128], o[:], R=[o], W=[concat])
    return


def build_post_a(C, NT, NCT, CONCAT_FM=False):
    S = C.S
    TT = NT * 128
    xin = C.din("xin", [TT, D])
    concat = C.din("concat")
    c2 = C.din("c2", [2, D])
    wmod = C.din("wmod", [D, 3 * D])
    bmod = C.din("bmod", [3 * D])
    nw = C.din("nw", [D])
    wout = C.din("wout", [D, D])
    wq = C.din("wq", [D, 2048])
    skT = C.din("skT", [128, 16 * 128])
    x1o = C.dout("x1", [TT, D])
    h2To = C.dout("h2T", [D, TT], BF16)
    listo = C.dout("lists", [3 * 128, TT], BF16)

    emit_consts(C)
    psT = [C.ps("psT%d" % i, [128, 8, 128], BF16) for i in range(2)]
    psQ = [C.ps("psQ%d" % i, [128, 512], F32) for i in range(4)]
    psL = C.ps("psL", [128, 3, 128], F32)

    wob = C.sb("wob", [128, 8, D], BF16)
    wqb = C.sb("wqb", [128, 8, 2048], BF16)
    skb = C.sb("skb", [128, 16 * 128], BF16)
    wov = wout[:].rearrange("(k p) n -> p k n", p=128)
    wqv = wq[:].rearrange("(k p) n -> p k n", p=128)
    for k in range(8):
        S.dma(wob[:, k, :], wov[:, k, :], R=[wout], W=[wob], queue="pool")
        for c0 in (0, 1024):
            S.dma(wqb[:, k, c0:c0 + 1024], wqv[:, k, c0:c0 + 1024], R=[wq], W=[wqb], queue="pool")
    for c0 in (0, 1024):
        S.dma(skb[:, c0:c0 + 1024], skT[:, c0:c0 + 1024], R=[skT], W=[skb], queue="pool")
    nwb = C.sb("nwb", [128, D], F32)
    S.dma(nwb[:], nw[:].partition_broadcast(128), R=[nw], W=[nwb])
    iota16 = C.sb("iota16", [128, 16], F32)
    S.op("pool", lambda e: e.iota(iota16[:], pattern=[[1, 16]], base=0, channel_multiplier=0,
                                  allow_small_or_imprecise_dtypes=True), W=[iota16])

    scrA = C.sb("scrA", [128, 4096], F32)
    scrB = C.sb("scrB", [128, 4096], F32)
    scrC = C.sb("scrC", [128, 3072], F32)
    wm = [C.view(b, b[:].rearrange("p (k n) -> p k n", k=8)) for b in (scrA, scrB)]
    modbc = emit_mod(C, c2, wmod, bmod, [2, 3, 4], psQ[2], psQ[3], wm, scrC)
    C.modres = modbc.r
    for v in range(2):
        S.op("dve", lambda e, v=v: e.scalar_tensor_tensor(
            out=modbc[:, v, 2, :], in0=modbc[:, v, 2, :], scalar=1.0, in1=nwb[:], op0=ALU.add, op1=ALU.mult),
            R=[modbc, nwb], W=[modbc])

    xts = [C.sb("xt%d" % i, [128, D], F32) for i in range(2)]
    cts = [C.sb("ct%d" % i, [128, D], F32) for i in range(2)]
    ctb = C.sb("ctb", [128, D], BF16)
    cT = C.sb("cT", [128, 8, 128], BF16)
    x1s = [C.sb("x1_%d" % i, [128, D], F32) for i in range(2)]
    scr = C.sb("scr", [128, D], F32)
    h2b = C.sb("h2b", [128, D], BF16)
    sss = [C.sb("ss%d" % i, [128, 4], F32) for i in range(2)]
    h2Ts = [C.sb("h2T%d" % i, [128, 8, 128], BF16) for i in range(2)]
    qTs = C.sb("qTs", [128, 16, 128], BF16)
    s_sb = C.view(scrA, scrA[:, 0:2048].rearrange("p (a k) -> p a k", a=16))
    s_wk = C.view(scrB, scrB[:, 0:2048].rearrange("p (a k) -> p a k", a=16))
    topv = C.sb("topv", [128, 16, 16], F32)
    topi = C.sb("topi", [128, 16, 16], U32)
    topif = C.sb("topif", [128, 16, 16], F32)
    cand = C.view(scrA, scrA[:, 2048:4096].rearrange("p (a k) -> p a k", a=8))
    cand_wk = C.view(scrB, scrB[:, 2048:4096].rearrange("p (a k) -> p a k", a=8))
    tops = C.sb("tops", [128, 8, 16], F32)
    pos = C.sb("pos", [128, 8, 16], U32)
    pa = C.sb("pa", [128, 8, 16], U32)
    pb = C.sb("pb", [128, 8, 16], U32)
    paf = C.sb("paf", [128, 8, 16], F32)
    pbf = C.sb("pbf", [128, 8, 16], F32)
    oh = C.view(scrC, scrC[:, 0:2048].rearrange("p (h a b) -> p h a b", h=8, a=16))
    gsm = C.sb("gsm", [128, 32], F32)
    eg = C.sb("eg", [128, 8, 16], F32)
    L = C.sb("L", [128, 3, 128], F32)
    Lb = [C.sb("Lb%d" % i, [128, 3, 128], BF16) for i in range(2)]
    h2v = h2To[:].rearrange("(k p) t -> p k t", p=128)
    lv = listo[:].rearrange("(a p) t -> p a t", p=128)

    for t in range(NT):
        v = 1 if t < NCT else 0
        xt, ct, x1, ss, h2T = xts[t % 2], cts[t % 2], x1s[t % 2], sss[t % 2], h2Ts[t % 2]
        rows = slice(t * 128, (t + 1) * 128)
        S.dma(xt[:], xin[rows, :], R=[xin], W=[xt])
        if CONCAT_FM:
            S.dma(ct[:].rearrange("p (k t) -> p k t", k=8), concat[:].rearrange("(k p) t -> p k t", p=128)[:, :, rows],
                  R=[concat], W=[ct])
            S.op("act", lambda e, ct=ct: e.activation(out=cT[:], in_=ct[:].rearrange("p (k t) -> p k t", k=8), func=AF.Copy),
                 R=[ct], W=[cT])
        else:
            S.dma(ct[:], concat[rows, :], R=[concat], W=[ct])
            S.op("act", lambda e, ct=ct: e.activation(out=ctb[:], in_=ct[:], func=AF.Copy), R=[ct], W=[ctb])
            pT = psT[0]
            for k in range(8):
                S.op("pe", lambda e, pT=pT, k=k: e.transpose(out=pT[:, k, :], in_=ctb[:, k * 128:(k + 1) * 128],
                                                             identity=C.identb[:]), R=[ctb, C.identb], W=[pT])
            S.op("dve", lambda e, pT=pT: e.tensor_copy(out=cT[:], in_=pT[:]), R=[pT], W=[cT])
        for half in range(2):
            py = psQ[half]
            for k in range(8):
                S.op("pe", lambda e, py=py, k=k, half=half: e.matmul(
                    py[:, :], lhsT=cT[:, k, :], rhs=wob[:, k, half * 512:(half + 1) * 512],
                    start=(k == 0), stop=(k == 7)), R=[cT, wob], W=[py])
            hs = slice(half * 512, (half + 1) * 512)
            S.op("dve", lambda e, py=py, hs=hs, v=v: e.tensor_tensor(
                out=scr[:, hs], in0=py[:, :], in1=modbc[:, v, 0, hs], op=ALU.mult), R=[py, modbc], W=[scr])
        S.op("pool", lambda e, x1=x1, xt=xt: e.tensor_tensor(out=x1[:], in0=scr[:], in1=xt[:], op=ALU.add),
             R=[scr, xt], W=[x1])
        S.dma(x1o[rows, :], x1[:], R=[x1], W=[x1o])
        emit_rmsnorm_mod(C, x1, modbc[:, v, 2, :], modbc[:, v, 1, :], h2b, "p", scr, ss)
        pT = psT[1]
        for k in range(8):
            S.op("pe", lambda e, pT=pT, k=k: e.transpose(out=pT[:, k, :], in_=h2b[:, k * 128:(k + 1) * 128],
                                                         identity=C.identb[:]), R=[h2b, C.identb], W=[pT])
        S.op("act", lambda e, pT=pT, h2T=h2T: e.activation(out=h2T[:], in_=pT[:], func=AF.Copy), R=[pT], W=[h2T])
        S.dma(h2v[:, :, rows], h2T[:], R=[h2T], W=[h2To])
        for hp in range(16):
            pq = psQ[hp // 4]
            for k in range(8):
                S.op("pe", lambda e, pq=pq, hp=hp, k=k, h2T=h2T: e.matmul(
                    pq[:, (hp % 4) * 128:(hp % 4 + 1) * 128], lhsT=wqb[:, k, hp * 128:(hp + 1) * 128], rhs=h2T[:, k, :],
                    start=(k == 0), stop=(k == 7)), R=[wqb, h2T], W=[pq])
        for q4 in range(4):
            S.op("act", lambda e, q4=q4: e.activation(
                out=qTs[:, q4 * 4:(q4 + 1) * 4, :], in_=psQ[q4][:, :].rearrange("p (a t) -> p a t", a=4), func=AF.Copy),
                R=[psQ[q4]], W=[qTs])
        for hp in range(16):
            pq = psQ[hp // 4]
            S.op("pe", lambda e, pq=pq, hp=hp: e.matmul(
                pq[:, (hp % 4) * 128:(hp % 4 + 1) * 128], lhsT=qTs[:, hp, :], rhs=skb[:, hp * 128:(hp + 1) * 128],
                start=True, stop=True), R=[qTs, skb], W=[pq])
        for q4 in range(4):
            S.op("act", lambda e, q4=q4: e.activation(
                out=s_sb[:, q4 * 4:(q4 + 1) * 4, :], in_=psQ[q4][:, :].rearrange("p (a t) -> p a t", a=4), func=AF.Copy),
                R=[psQ[q4]], W=[s_sb])

        def top16(src, wk, vals, idx, n):
            for a in range(n):
                S.op("dve", lambda e, a=a: e.max(out=vals[:, a, 0:8], in_=src[:, a, :]), R=[src], W=[vals])
                S.op("dve", lambda e, a=a: e.max_index(out=idx[:, a, 0:8], in_max=vals[:, a, 0:8], in_values=src[:, a, :]),
                     R=[src, vals], W=[idx])
                S.op("dve", lambda e, a=a: e.match_replace(out=wk[:, a, :], in_to_replace=vals[:, a, 0:8],
                                                           in_values=src[:, a, :], imm_value=NEG), R=[src, vals], W=[wk])
                S.op("dve", lambda e, a=a: e.max(out=vals[:, a, 8:16], in_=wk[:, a, :]), R=[wk], W=[vals])
                S.op("dve", lambda e, a=a: e.max_index(out=idx[:, a, 8:16], in_max=vals[:, a, 8:16], in_values=wk[:, a, :]),
                     R=[wk, vals], W=[idx])
        top16(s_sb, s_wk, topv, topi, 16)
        tv4 = topv[:].rearrange("p (h two) a -> p h two a", two=2)
        S.op("dve", lambda e: e.tensor_tensor(
            out=cand[:].rearrange("p h (a b) -> p h a b", a=16),
            in0=tv4[:, :, 0, :].unsqueeze(3).to_broadcast([128, 8, 16, 16]),
            in1=tv4[:, :, 1, :].unsqueeze(2).to_broadcast([128, 8, 16, 16]), op=ALU.add), R=[topv], W=[cand])
        top16(cand, cand_wk, tops, pos, 8)
        S.op("dve", lambda e: e.tensor_scalar(out=gsm[:, 0:8], in0=tops[:, :, 0], scalar1=-1.0, scalar2=None, op0=ALU.mult),
             R=[tops], W=[gsm])
        for h in range(8):
            S.op("act", lambda e, h=h: e.activation(out=eg[:, h, :], in_=tops[:, h, :], func=AF.Exp,
                                                    bias=gsm[:, h:h + 1], scale=1.0, accum_out=gsm[:, 8 + h:9 + h]),
                 R=[tops, gsm], W=[eg, gsm])
        S.op("dve", lambda e: e.reciprocal(out=gsm[:, 16:24], in_=gsm[:, 8:16]), R=[gsm], W=[gsm])
        S.op("dve", lambda e: e.tensor_tensor(out=L[:, 2, :].rearrange("p (h k) -> p h k", h=8), in0=eg[:],
                                              in1=gsm[:, 16:24].unsqueeze(2).to_broadcast([128, 8, 16]), op=ALU.mult),
             R=[eg, gsm], W=[L])
        S.op("dve", lambda e: e.tensor_single_scalar(out=pa[:], in_=pos[:], scalar=4, op=ALU.logical_shift_right),
             R=[pos], W=[pa])
        S.op("dve", lambda e: e.tensor_single_scalar(out=pb[:], in_=pos[:], scalar=15, op=ALU.bitwise_and),
             R=[pos], W=[pb])
        S.op("dve", lambda e: e.tensor_copy(out=paf[:], in_=pa[:]), R=[pa], W=[paf])
        S.op("dve", lambda e: e.tensor_copy(out=pbf[:], in_=pb[:]), R=[pb], W=[pbf])
        S.op("dve", lambda e: e.tensor_copy(out=topif[:], in_=topi[:]), R=[topi], W=[topif])
        ti4 = topif[:].rearrange("p (h two) a -> p h two a", two=2)
        for which, pf in ((0, paf), (1, pbf)):
            S.op("dve", lambda e, pf=pf: e.tensor_tensor(
                out=oh[:], in0=pf[:].unsqueeze(3).to_broadcast([128, 8, 16, 16]),
                in1=iota16[:].unsqueeze(1).unsqueeze(1).to_broadcast([128, 8, 16, 16]), op=ALU.is_equal),
                R=[pf, iota16], W=[oh])
            S.op("dve", lambda e, which=which: e.tensor_tensor(
                out=oh[:], in0=oh[:], in1=ti4[:, :, which, :].unsqueeze(2).to_broadcast([128, 8, 16, 16]), op=ALU.mult),
                R=[oh, topif], W=[oh])
            S.op("dve", lambda e, which=which: e.tensor_reduce(
                out=L[:, which, :].rearrange("p (h k) -> p h k", h=8), in_=oh[:], op=ALU.add, axis=AX.X),
                R=[oh], W=[L])
        for a in range(3):
            S.op("pe", lambda e, a=a: e.transpose(out=psL[:, a, :], in_=L[:, a, :], identity=C.identf[:]),
                 R=[L, C.identf], W=[psL])
        lb = Lb[t % 2]
        S.op("act", lambda e, lb=lb: e.activation(out=lb[:], in_=psL[:], func=AF.Copy), R=[psL], W=[lb])
        S.dma(lv[:, :, rows], lb[:], R=[lb], W=[listo])
    return


NEXP = 16384


def build_cast():
    S = C.S
    UT = C.din("UT", [D, NEXP])
    V = C.din("V", [NEXP, D])
    UTb = C.dout("UTb", [D, NEXP], BF16)
    Vb = C.dout("Vb", [NEXP, D], BF16)
    for (src, dst) in ((UT, UTb), (V, Vb)):
        sv = src[:].rearrange("a (b c) -> (a b) c", c=2048) if src is UT else src[:].rearrange("(a b) c -> a (b c)", b=2)
        dvv = dst[:].rearrange("a (b c) -> (a b) c", c=2048) if src is UT else dst[:].rearrange("(a b) c -> a (b c)", b=2)
        for r0 in range(0, 8192, 256):
            S.dma(dvv[r0:r0 + 256, :], sv[r0:r0 + 256, :], R=[src], W=[dst], queue="pool")
    return


def build_post_b(C, NT, NCT, FINAL=False):
    S = C.S
    TT = NT * 128
    NG = NT // 2
    x1i = C.din("x1", [TT, D])
    h2Ti = C.din("h2T", [D, TT], BF16)
    listi = C.din("lists", [3 * 128, TT], BF16)
    UTf = C.din("UT", [D, NEXP])
    Vf = C.din("V", [NEXP, D])
    UTb = C.din("UTb", [D, NEXP], BF16)
    Vb = C.din("Vb", [NEXP, D], BF16)
    c2 = C.din("c2", [2, D])
    wmod = C.din("wmod", [D, 1 * D])
    bmod = C.din("bmod", [1 * D])
    fnw = C.din("fnw", [D])
    xout = C.dout("xout", [TT, D])
    utres = [Res("utb%d" % i) for i in range(32)]
    vbres = [Res("vbb%d" % i) for i in range(32)]
    for blk in range(32):
        e0 = blk * 512
        S.dma(UTb[:, e0:e0 + 512], UTf[:, e0:e0 + 512], R=[UTf], W=[utres[blk]], queue="pool")
        S.dma(Vb[e0:e0 + 512, :], Vf[e0:e0 + 512, :], R=[Vf], W=[vbres[blk]], queue="pool")

    emit_consts(C)
    acc = [C.ps("acc%d" % i, [128, 512], F32) for i in range(4)]
    psA = [C.ps("psA%d" % i, [128, 512], F32) for i in range(2)]
    psG = [C.ps("psG%d" % i, [128, 4, 128], F32) for i in range(2)]

    GTraw = C.sb("GTraw", [128, 128 * 256], BF16)
    GT = C.view(GTraw, GTraw[:].rearrange("p (i t) -> p i t", i=128))
    wm = [C.view(GTraw, GTraw[:].bitcast(F32)[:, i * 4096:(i + 1) * 4096].rearrange("p (k n) -> p k n", k=8)) for i in range(2)]
    bmv = C.view(GTraw, GTraw[:].bitcast(F32)[:, 8192:8192 + 1024])
    modbc = emit_mod(C, c2, wmod, bmod, [5], psA[0], psA[1], wm, bmv)
    C.modres = modbc.r
    fnb = C.sb("fnb", [128, D], F32)
    S.dma(fnb[:], fnw[:].partition_broadcast(128), R=[fnw], W=[fnb])
    iota = C.sb("iota", [128, 128], BF16)
    S.op("pool", lambda e: e.iota(iota[:], pattern=[[1, 128]], base=0, channel_multiplier=0,
                                  allow_small_or_imprecise_dtypes=True), W=[iota])

    uts = [C.sb("ut%d" % i, [128, 8, 512], BF16) for i in range(2)]
    vbs = [C.sb("vb%d" % i, [128, 4, D], BF16) for i in range(2)]
    h2s = [C.sb("h2g%d" % i, [128, 8, 256], BF16) for i in range(2)]
    lgs = [C.sb("lg%d" % i, [128, 3, 256], BF16) for i in range(2)]
    Qs = [C.sb("Q%d" % i, [128, 32, 128], BF16) for i in range(2)]
    P0 = C.sb("Pz", [128, 32, 128], BF16)
    Ps = [C.sb("P%d" % i, [128, 32, 128], BF16) for i in range(2)]
    gas = [C.sb("ga%d" % i, [128, 2, 256], F32) for i in range(2)]
    cfs = [C.sb("cf%d" % i, [128, 4, 256], BF16) for i in range(2)]
    x1s = [C.sb("x1_%d" % i, [128, D], F32) for i in range(2)]
    scr = C.sb("scr", [128, D], F32)
    x2s = [C.sb("x2_%d" % i, [128, D], F32) for i in range(2)]
    sss = [C.sb("ss%d" % i, [128, 4], F32) for i in range(2)]
    h2v = h2Ti[:].rearrange("(k p) t -> p k t", p=128)
    lv = listi[:].rearrange("(a p) t -> p a t", p=128)
    utv = UTb[:].rearrange("(k p) e -> p k e", p=128)
    vv = Vb[:].rearrange("(i j) d -> j i d", j=128)
    nblk = 0
    nsub = 0
    nev = 0
    for g in range(NG):
        tok0 = g * 256
        v = 1 if g * 2 < NCT else 0
        lg, h2g = lgs[g % 2], h2s[g % 2]
        S.dma(lg[:], lv[:, :, tok0:tok0 + 256], R=[listi], W=[lg])
        S.dma(h2g[:], h2v[:, :, tok0:tok0 + 256], R=[h2Ti], W=[h2g])
        for sbi in range(8):
            t0 = sbi * 32
            Q, P = Qs[nsub % 2], Ps[nsub % 2]
            nsub += 1
            S.op("dve", lambda e, Q=Q, lg=lg, t0=t0: e.tensor_tensor(
                out=Q[:], in0=lg[:, 1, t0:t0 + 32].unsqueeze(2).to_broadcast([128, 32, 128]),
                in1=iota[:].unsqueeze(1).to_broadcast([128, 32, 128]), op=ALU.is_equal), R=[lg, iota], W=[Q])
            S.op("dve", lambda e, lg=lg, t0=t0: e.tensor_tensor(
                out=P0[:], in0=lg[:, 0, t0:t0 + 32].unsqueeze(2).to_broadcast([128, 32, 128]),
                in1=iota[:].unsqueeze(1).to_broadcast([128, 32, 128]), op=ALU.is_equal), R=[lg, iota], W=[P0])
            S.op("pool", lambda e, P=P, lg=lg, t0=t0: e.tensor_tensor(
                out=P[:], in0=P0[:], in1=lg[:, 2, t0:t0 + 32].unsqueeze(2).to_broadcast([128, 32, 128]), op=ALU.mult),
                R=[P0, lg], W=[P])
            for t4 in range(8):
                pg = psG[nev % 2]
                for u in range(4):
                    tt = t4 * 4 + u
                    S.op("pe", lambda e, pg=pg, u=u, tt=tt, Q=Q, P=P: e.matmul(
                        pg[:, u, :], lhsT=Q[:, tt, :], rhs=P[:, tt, :], start=True, stop=True), R=[Q, P], W=[pg])
                tg = t0 + t4 * 4
                eng = "act" if nev % 2 == 0 else "dve"
                if eng == "act":
                    S.op("act", lambda e, pg=pg, tg=tg: e.activation(
                        out=GT[:, :, tg:tg + 4].rearrange("p i t -> p t i"), in_=pg[:], func=AF.Copy), R=[pg], W=[GT])
                else:
                    S.op("dve", lambda e, pg=pg, tg=tg: e.tensor_copy(
                        out=GT[:, :, tg:tg + 4].rearrange("p i t -> p t i"), in_=pg[:]), R=[pg], W=[GT])
                nev += 1
        for blk in range(32):
            ut, vb = uts[nblk % 2], vbs[nblk % 2]
            nblk += 1
            e0 = blk * 512
            S.dma(ut[:], utv[:, :, e0:e0 + 512], R=[utres[blk]], W=[ut])
            S.dma(vb[:], vv[:, blk * 4:(blk + 1) * 4, :], R=[vbres[blk]], W=[vb])
            cf = cfs[blk % 2]
            for cp in range(2):
                pa = psA[cp]
                ga = gas[cp]
                for ii in range(2):
                    ch = cp * 2 + ii
                    for k in range(8):
                        S.op("pe", lambda e, pa=pa, ii=ii, ch=ch, k=k, ut=ut, h2g=h2g: e.matmul(
                            pa[:, ii * 256:(ii + 1) * 256], lhsT=ut[:, k, ch * 128:(ch + 1) * 128], rhs=h2g[:, k, :],
                            start=(k == 0), stop=(k == 7)), R=[ut, h2g], W=[pa])
                S.op("act", lambda e, pa=pa, ga=ga: e.activation(
                    out=ga[:].rearrange("p a t -> p (a t)"), in_=pa[:, :], func=AF.Gelu), R=[pa], W=[ga])
                i0 = blk * 4 + cp * 2
                eng = "dve" if cp == 0 else "pool"
                S.op(eng, lambda e, cf=cf, ga=ga, cp=cp, i0=i0: e.tensor_tensor(
                    out=cf[:, cp * 2:cp * 2 + 2, :], in0=ga[:], in1=GT[:, i0:i0 + 2, :], op=ALU.mult),
                    R=[ga, GT], W=[cf])
            for ii in range(4):
                for tt in range(2):
                    for half in range(2):
                        a = acc[tt * 2 + half]
                        S.op("pe", lambda e, a=a, ii=ii, tt=tt, half=half, cf=cf, vb=vb, blk=blk: e.matmul(
                            a[:, :], lhsT=cf[:, ii, tt * 128:(tt + 1) * 128], rhs=vb[:, ii, half * 512:(half + 1) * 512],
                            start=(blk == 0 and ii == 0), stop=(blk == 31 and ii == 3)), R=[cf, vb], W=[a])
        for tt in range(2):
            t = g * 2 + tt
            rows = slice(t * 128, (t + 1) * 128)
            x1, x2, ss = x1s[tt], x2s[tt], sss[tt]
            S.dma(x1[:], x1i[rows, :], R=[x1i], W=[x1])
            for half in range(2):
                hs = slice(half * 512, (half + 1) * 512)
                S.op("dve", lambda e, a=acc[tt * 2 + half], hs=hs, v=v: e.tensor_tensor(
                    out=scr[:, hs], in0=a[:, :], in1=modbc[:, v, 0, hs], op=ALU.mult), R=[acc[tt * 2 + half], modbc], W=[scr])
            S.op("pool", lambda e, x1=x1, x2=x2: e.tensor_tensor(out=x2[:], in0=scr[:], in1=x1[:], op=ALU.add),
                 R=[scr, x1], W=[x2])
            if FINAL:
                emit_rmsnorm_mod(C, x2, fnb[:], None, x1, "fin", scr, ss)
                S.dma(xout[rows, :], x1[:], R=[x1], W=[xout])
            else:
                S.dma(xout[rows, :], x2[:], R=[x2], W=[xout])
    return


def build_f_odd(C, NT, NCT):
    S = C.S
    TT = NT * 128
    NCH = 12
    xin = C.din("xin", [TT, D])
    c2 = C.din("c2", [2, D])
    wmod = C.din("wmod", [D, 2 * D])
    bmod = C.din("bmod", [2 * D])
    nw = C.din("nw", [D])
    wall = C.din("wall", [D, NCH * 128])
    pTo = C.dout("pT", [NCH * 128, TT])

    emit_consts(C)
    psT = [C.ps("psT%d" % i, [128, 8, 128], BF16) for i in range(2)]
    psP = [C.ps("psP%d" % i, [128, 512], F32) for i in range(4)]
    wsb = C.sb("wsb", [128, 8, NCH * 128], BF16)
    wv = wall[:].rearrange("(k p) n -> p k n", p=128)
    for k in range(8):
        for c0 in range(0, NCH * 128, 768):
            S.dma(wsb[:, k, c0:c0 + 768], wv[:, k, c0:c0 + 768], R=[wall], W=[wsb], queue="pool")
    nwb = C.sb("nwb", [128, D], F32)
    S.dma(nwb[:], nw[:].partition_broadcast(128), R=[nw], W=[nwb])
    scrA = C.sb("scrA", [128, 4096], F32)
    scrB = C.sb("scrB", [128, 4096], F32)
    scrC = C.sb("scrC", [128, 2048], F32)
    wm = [C.view(b, b[:].rearrange("p (k n) -> p k n", k=8)) for b in (scrA, scrB)]
    modbc = emit_mod(C, c2, wmod, bmod, [0, 1], psP[0], psP[1], wm, scrC)
    C.modres = modbc.r
    for v in range(2):
        S.op("dve", lambda e, v=v: e.scalar_tensor_tensor(
            out=modbc[:, v, 1, :], in0=modbc[:, v, 1, :], scalar=1.0, in1=nwb[:], op0=ALU.add, op1=ALU.mult),
            R=[modbc, nwb], W=[modbc])
    xts = [C.sb("xt%d" % i, [128, D], F32) for i in range(2)]
    scr = C.view(scrC, scrC[:, 0:D])
    hbs = [C.sb("hb%d" % i, [128, D], BF16) for i in range(2)]
    sss = [C.sb("ss%d" % i, [128, 4], F32) for i in range(2)]
    hTs = [C.sb("hT%d" % i, [128, 8, 256], BF16) for i in range(2)]
    pos_ = [C.view(b, b[:, 0:NCH * 256].rearrange("p (c t) -> p c t", c=NCH)) for b in (scrA, scrB)]
    pv = pTo[:].rearrange("(c p) t -> p c t", p=128)
    for b in range(NT // 2):
        hT = hTs[b % 2]
        for tl in range(2):
            t = b * 2 + tl
            v = 1 if t < NCT else 0
            xt, hb, ss = xts[t % 2], hbs[t % 2], sss[t % 2]
            S.dma(xt[:], xin[t * 128:(t + 1) * 128, :], R=[xin], W=[xt])
            emit_rmsnorm_mod(C, xt, modbc[:, v, 1, :], modbc[:, v, 0, :], hb, "f", scr, ss)
            pT = psT[t % 2]
            for k in range(8):
                S.op("pe", lambda e, pT=pT, hb=hb, k=k: e.transpose(
                    out=pT[:, k, :], in_=hb[:, k * 128:(k + 1) * 128], identity=C.identb[:]), R=[hb, C.identb], W=[pT])
            S.op("act", lambda e, pT=pT, hT=hT, tl=tl: e.activation(
                out=hT[:, :, tl * 128:(tl + 1) * 128], in_=pT[:, :, :], func=AF.Copy), R=[pT], W=[hT])
        po = pos_[b % 2]
        tok0 = b * 256
        for cp in range(NCH // 2):
            pp = psP[cp % 4]
            for ii in range(2):
                ch = cp * 2 + ii
                for k in range(8):
                    S.op("pe", lambda e, pp=pp, hT=hT, k=k, ch=ch, ii=ii: e.matmul(
                        pp[:, ii * 256:(ii + 1) * 256], lhsT=wsb[:, k, ch * 128:(ch + 1) * 128], rhs=hT[:, k, :],
                        start=(k == 0), stop=(k == 7)), R=[hT, wsb], W=[pp])
            if cp % 2 == 0:
                S.op("act", lambda e, pp=pp, po=po, cp=cp: e.activation(
                    out=po[:, cp * 2:cp * 2 + 2, :], in_=pp[:, :].rearrange("p (a t) -> p a t", a=2), func=AF.Copy),
                    R=[pp], W=[po])
            else:
                S.op("dve", lambda e, pp=pp, po=po, cp=cp: e.tensor_copy(
                    out=po[:, cp * 2:cp * 2 + 2, :], in_=pp[:, :].rearrange("p (a t) -> p a t", a=2)), R=[pp], W=[po])
        S.dma(pv[:, :, tok0:tok0 + 256], po[:], R=[po], W=[pTo])
    return


def rec_layout(NL):
    LB = 16 + 256 + 32 + NL + 16
    return LB, 16, 304


def build_rec(C, NL, half):
    S = C.S
    LB, OC, OL = rec_layout(NL)
    TK = 256 + NL
    xpT = C.din("xpT", [512, TK])
    xrT = C.din("xrT", [256, TK])
    gTi = C.din("gT", [256, TK])
    pw = C.din("pw", [4, 128, 64])
    psc = C.din("psc", [64, 4])
    rcnt = C.din("rcnt", [4, LB])
    cw = C.din("cw", [256, 5])
    wbd = C.din("wbd", [2, 2, 2, 128, 128])
    gb = C.din("gb", [256, 6])
    recT = C.dout("concatT", [1024, TK])

    emit_consts(C)
    psm = [C.ps("psm%d" % i, [128, 512], F32) for i in range(6)]
    bufs = [C.sb("pb%d" % i, [128, LB], F32) for i in range(3)]
    rc = C.sb("rc", [128, LB], F32)
    pwb = C.sb("pwb", [128, 4, 64], F32)
    pscb = C.sb("pscb", [64, 4], F32)
    S.dma(pwb[:], pw[:].rearrange("g d e -> d g e"), R=[pw], W=[pwb])
    S.dma(pscb[:], psc[:], R=[psc], W=[pscb])
    yo = [C.sb("yo%d" % i, [64, 512], F32) for i in range(2)]
    blocks = [(OC, 0, 256)] + [(OL + k * 512, 256 + k * 512, 512) for k in range(NL // 512)]
    nmm = 0
    for g in range(4):
        x0, sA, sB = bufs[0], bufs[1], bufs[2]
        S.op("pool", lambda e, x0=x0: e.memset(x0[:], 0.0), W=[x0])
        S.dma(x0[:, OC:OC + 256], xpT[g * 128:(g + 1) * 128, 0:256], R=[xpT], W=[x0])
        S.dma(x0[:, OL:OL + NL], xpT[g * 128:(g + 1) * 128, 256:256 + NL], R=[xpT], W=[x0])
        S.dma(rc[:], rcnt[g].partition_broadcast(128), R=[rcnt], W=[rc])
        src = x0
        dsts = [sA, sB]
        for lvl in range(g + 1):
            sh = 1 << max(lvl - 1, 0)
            dst = dsts[lvl % 2]
            lo, hi = 8, LB - 8
            if lvl == 0:
                S.op("dve", lambda e, src=src, dst=dst, lo=lo, hi=hi: e.tensor_tensor(
                    out=dst[:, lo:hi], in0=src[:, lo - 1:hi - 1], in1=src[:, lo:hi], op=ALU.add), R=[src], W=[dst])
            else:
                S.op("dve", lambda e, src=src, dst=dst, lo=lo, hi=hi, sh=sh: e.tensor_tensor(
                    out=dst[:, lo:hi], in0=src[:, lo - sh:hi - sh], in1=src[:, lo + sh:hi + sh], op=ALU.add), R=[src], W=[dst])
            src = dst
        dd = dsts[(g + 1) % 2]
        S.op("pool", lambda e, src=src, dd=dd: e.tensor_tensor(out=dd[:, 16:LB - 16], in0=src[:, 16:LB - 16],
                                                              in1=rc[:, 16:LB - 16], op=ALU.mult), R=[src, rc], W=[dd])
        S.op("dve", lambda e, dd=dd, x0=x0: e.tensor_tensor(out=dd[:, 16:LB - 16], in0=dd[:, 16:LB - 16],
                                                             in1=x0[:, 16:LB - 16], op=ALU.subtract), R=[dd, x0], W=[dd])
        for (bo, to, n) in blocks:
            pm = psm[nmm % 2]
            y = yo[nmm % 2]
            nmm += 1
            S.op("pe", lambda e, pm=pm, dd=dd, bo=bo, n=n, g=g: e.matmul(
                pm[0:64, 0:n], lhsT=pwb[:, g, :], rhs=dd[:, bo:bo + n], start=True, stop=True), R=[pwb, dd], W=[pm])
            S.op("act", lambda e, pm=pm, y=y, n=n, g=g: e.activation(
                out=y[:, 0:n], in_=pm[0:64, 0:n], func=AF.Copy, scale=pscb[:, g:g + 1]), R=[pm, pscb], W=[y])
            S.dma(recT[g * 128 + half * 64:g * 128 + half * 64 + 64, to:to + n], y[:, 0:n], R=[y], W=[recT])

    xr = bufs[0]
    ub = bufs[1]
    hf = bufs[2]
    hbk = rc
    cwb = C.sb("cwb", [128, 2, 5], F32)
    gbb = C.sb("gbb", [128, 2, 6], F32)
    nsp = C.sb("nsp", [128, 2, 4], F32)
    S.dma(cwb[:], cw[:].rearrange("(c p) k -> p c k", p=128), R=[cw], W=[cwb])
    S.dma(gbb[:], gb[:].rearrange("(c p) k -> p c k", p=128), R=[gb], W=[gbb])
    wb = C.sb("wbd_sb", [128, 8, 128], F32)
    S.dma(wb[:], wbd[:].rearrange("g r c d e -> d (g r c) e"), R=[wbd], W=[wb])
    S.op("act", lambda e: e.activation(out=nsp[:, :, 0:2], in_=gbb[:, :, 4:6], func=AF.Exp, scale=-1.0), R=[gbb], W=[nsp])
    S.op("act", lambda e: e.activation(out=nsp[:, :, 2:4], in_=nsp[:, :, 0:2], func=AF.Ln, bias=1.0, scale=1.0), R=[nsp], W=[nsp])
    S.op("dve", lambda e: e.tensor_scalar(out=nsp[:, :, 0:2], in0=nsp[:, :, 2:4], scalar1=-8.0, scalar2=None, op0=ALU.mult),
         R=[nsp], W=[nsp])
    tr = [C.sb("tr%d" % i, [128, 512], F32) for i in range(2)]
    ti = [C.sb("ti%d" % i, [128, 512], F32) for i in range(2)]
    ta = [C.sb("ta%d" % i, [128, 512], F32) for i in range(2)]
    tb = [C.sb("tb%d" % i, [128, 512], F32) for i in range(2)]
    tg = [C.sb("tg%d" % i, [128, 512], F32) for i in range(2)]
    to_ = [C.sb("to%d" % i, [128, 512], F32) for i in range(2)]
    nb_ = 0
    for c in range(2):
        S.op("pool", lambda e: e.memset(xr[:], 0.0), W=[xr])
        S.dma(xr[:, OC:OC + 256], xrT[c * 128:(c + 1) * 128, 0:256], R=[xrT], W=[xr])
        S.dma(xr[:, OL:OL + NL], xrT[c * 128:(c + 1) * 128, 256:256 + NL], R=[xrT], W=[xr])
        lo, hi = 8, LB - 8
        S.op("dve", lambda e, c=c: e.tensor_scalar(out=ub[:, lo:hi], in0=xr[:, lo - 2:hi - 2], scalar1=cwb[:, c, 0:1],
                                                    scalar2=cwb[:, c, 4:5], op0=ALU.mult, op1=ALU.add), R=[xr, cwb], W=[ub])
        for k in range(1, 4):
            S.op("dve", lambda e, c=c, k=k: e.scalar_tensor_tensor(
                out=ub[:, lo:hi], in0=xr[:, lo + k - 2:hi + k - 2], scalar=cwb[:, c, k:k + 1], in1=ub[:, lo:hi],
                op0=ALU.mult, op1=ALU.add), R=[xr, cwb, ub], W=[ub])
        for d in range(2):
            hd = hf if d == 0 else hbk
            order = blocks if d == 0 else [blocks[0]] + blocks[:0:-1]
            prev = None
            for (bo, to, n) in order:
                i_ = nb_ % 2
                nb_ += 1
                pr, pi = psm[2 + (nb_ % 2) * 2], psm[3 + (nb_ % 2) * 2]
                S.op("pe", lambda e, pr=pr, bo=bo, n=n, d=d, c=c: e.matmul(
                    pr[:, 0:n], lhsT=wb[:, (0 * 2 + d) * 2 + c, :], rhs=ub[:, bo:bo + n], start=True, stop=True), R=[wb, ub], W=[pr])
                S.op("pe", lambda e, pi=pi, bo=bo, n=n, d=d, c=c: e.matmul(
                    pi[:, 0:n], lhsT=wb[:, (1 * 2 + d) * 2 + c, :], rhs=ub[:, bo:bo + n], start=True, stop=True), R=[wb, ub], W=[pi])
                r_, i2, a_, b_ = tr[i_], ti[i_], ta[i_], tb[i_]
                S.op("act", lambda e, pr=pr, r_=r_, n=n, d=d, c=c: e.activation(
                    out=r_[:, 0:n], in_=pr[:, 0:n], func=AF.Sigmoid, bias=gbb[:, c, d:d + 1], scale=1.0), R=[pr, gbb], W=[r_])
                S.op("act", lambda e, pi=pi, i2=i2, n=n, d=d, c=c: e.activation(
                    out=i2[:, 0:n], in_=pi[:, 0:n], func=AF.Sigmoid, bias=gbb[:, c, 2 + d:3 + d], scale=1.0), R=[pi, gbb], W=[i2])
                S.op("act", lambda e, r_=r_, a_=a_, n=n, d=d, c=c: e.activation(
                    out=a_[:, 0:n], in_=r_[:, 0:n], func=AF.Exp, scale=nsp[:, c, d:d + 1]), R=[r_, nsp], W=[a_])
                S.op("dve", lambda e, a_=a_, b_=b_, n=n: e.tensor_tensor(out=b_[:, 0:n], in0=a_[:, 0:n], in1=a_[:, 0:n], op=ALU.mult),
                     R=[a_], W=[b_])
                S.op("dve", lambda e, b_=b_, n=n: e.tensor_scalar(out=b_[:, 0:n], in0=b_[:, 0:n], scalar1=-1.0, scalar2=1.0,
                                                                  op0=ALU.mult, op1=ALU.add), R=[b_], W=[b_])
                S.op("act", lambda e, b_=b_, n=n: e.activation(out=b_[:, 0:n], in_=b_[:, 0:n], func=AF.Sqrt), R=[b_], W=[b_])
                S.op("pool", lambda e, i2=i2, bo=bo, n=n: e.tensor_tensor(out=i2[:, 0:n], in0=i2[:, 0:n], in1=ub[:, bo:bo + n], op=ALU.mult),
                     R=[i2, ub], W=[i2])
                S.op("dve", lambda e, b_=b_, i2=i2, n=n: e.tensor_tensor(out=b_[:, 0:n], in0=b_[:, 0:n], in1=i2[:, 0:n], op=ALU.mult),
                     R=[b_, i2], W=[b_])
                if prev is None:
                    init = 0.0
                else:
                    pbo, pn = prev
                    init = hd[:, pbo + pn - 1:pbo + pn] if d == 0 else hd[:, pbo:pbo + 1]
                if d == 0:
                    S.op("dve", lambda e, a_=a_, b_=b_, bo=bo, n=n, init=init, hd=hd: e.tensor_tensor_scan(
                        out=hd[:, bo:bo + n], data0=a_[:, 0:n], data1=b_[:, 0:n], initial=init, op0=ALU.mult, op1=ALU.add),
                        R=[a_, b_, hd], W=[hd])
                else:
                    S.op("dve", lambda e, a_=a_, b_=b_, bo=bo, n=n, init=init, hd=hd: e.tensor_tensor_scan(
                        out=hd[:, bo:bo + n][:, ::-1], data0=a_[:, 0:n][:, ::-1], data1=b_[:, 0:n][:, ::-1], initial=init,
                        op0=ALU.mult, op1=ALU.add), R=[a_, b_, hd], W=[hd])
                prev = (bo, n)
                if d == 1:
                    g_, o_ = tg[i_], to_[i_]
                    S.dma(g_[:, 0:n], gTi[c * 128:(c + 1) * 128, to:to + n], R=[gTi], W=[g_])
                    S.op("act", lambda e, g_=g_, n=n: e.activation(out=g_[:, 0:n], in_=g_[:, 0:n], func=AF.Gelu), R=[g_], W=[g_])
                    S.op("pool", lambda e, o_=o_, bo=bo, n=n: e.tensor_tensor(out=o_[:, 0:n], in0=hf[:, bo:bo + n], in1=hbk[:, bo:bo + n], op=ALU.add),
                         R=[hf, hbk], W=[o_])
                    S.op("pool", lambda e, o_=o_, g_=g_, n=n: e.tensor_tensor(out=o_[:, 0:n], in0=o_[:, 0:n], in1=g_[:, 0:n], op=ALU.mult),
                         R=[o_, g_], W=[o_])
                    S.dma(recT[512 + half * 256 + c * 128:512 + half * 256 + (c + 1) * 128, to:to + n], o_[:, 0:n], R=[o_], W=[recT])
    return


GRID_W = 64
ROPE_THETA = 10000.0
SEQ = 8192
CTX = 256
HALF = SEQ // 2
NT_CORE = (CTX + HALF) // 128
NCT_CORE = CTX // 128
bf16 = ml_dtypes.bfloat16


def _swap_halves_cols(w):
    n = w.shape[1] // 64
    w4 = w.reshape(w.shape[0], n, 2, 32)
    return np.ascontiguousarray(w4[:, :, ::-1, :]).reshape(w.shape[0], n * 64)


def _prep_wall_even(w_in):
    q = w_in[:, 0:512]; k = w_in[:, 512:640]; v = w_in[:, 640:768]
    dq = w_in[:, 768:1280]; dk = w_in[:, 1280:1792]; dv = w_in[:, 1792:2304]
    kd = np.concatenate([k[:, 0:64], k[:, 0:64], k[:, 64:128], k[:, 64:128]], axis=1)
    fm = np.concatenate([q, kd, dq, dk], axis=1)
    return np.ascontiguousarray(np.concatenate([fm, _swap_halves_cols(fm), v, dv], axis=1), dtype=np.float32)


def _rope_tables(tok_pos):
    inv = (ROPE_THETA ** (-np.arange(16, dtype=np.float32) / 16)).astype(np.float32)
    pos = np.maximum(tok_pos, 0)
    row = (pos // GRID_W).astype(np.float32)
    col = (pos % GRID_W).astype(np.float32)
    ang = np.concatenate([row[:, None] * inv, col[:, None] * inv], axis=-1).astype(np.float32)
    c = np.cos(ang).astype(np.float32)
    s = np.sin(ang).astype(np.float32)
    isctx = tok_pos < 0
    c[isctx] = 1.0
    s[isctx] = 0.0
    f = np.arange(128) % 32
    sign = np.where((np.arange(128) % 64) < 32, -1.0, 1.0).astype(np.float32)
    cosT = np.ascontiguousarray(c[:, f].T)
    sinT = np.ascontiguousarray((s[:, f] * sign[None, :]).T)
    return cosT, sinT


def _rec_rcnt(NL):
    LB = rec_layout(NL)[0]
    out = np.ones((4, LB), np.float32)
    for g, w in enumerate((2, 4, 8, 16)):
        for (off, Tn) in ((16, 256), (304, NL)):
            t = np.arange(Tn)
            lo = np.clip(t - w // 2, 0, Tn)
            hi = np.clip(t - w // 2 + w, 0, Tn)
            out[g, off:off + Tn] = 1.0 / (hi - lo).astype(np.float32)
    return out


def _rec_core_inputs(half, pool_w, pool_scale, conv_w, conv_b, wa, ba, wx, bx, lam):
    cs = slice(half * 256, (half + 1) * 256)
    pw = np.ascontiguousarray(pool_w[:, :, half * 64:(half + 1) * 64])
    psc = np.ascontiguousarray(pool_scale.reshape(4, 128)[:, half * 64:(half + 1) * 64].T)
    cw = np.ascontiguousarray(np.concatenate([conv_w[:, cs].T, conv_b[cs][:, None]], 1))
    wbd = np.zeros((2, 2, 2, 128, 128), np.float32)
    for gi, W in enumerate((wa, wx)):
        for d in range(2):
            for c in range(2):
                for bb in range(2):
                    blk = half * 4 + c * 2 + bb
                    wbd[gi, d, c, bb * 64:(bb + 1) * 64, bb * 64:(bb + 1) * 64] = W[d, blk]
    gb = np.ascontiguousarray(np.stack([ba[0, cs], ba[1, cs], bx[0, cs], bx[1, cs], lam[0, cs], lam[1, cs]], 1))
    return dict(pw=pw, psc=psc, cw=cw, wbd=wbd, gb=gb)


def _lambda_init(layer):
    return 0.8 - 0.6 * math.exp(-0.3 * layer)


_PROG_CACHE = {}
GRID_W = 64
ROPE_THETA = 10000.0
CTX = 256
bf16 = ml_dtypes.bfloat16


def build_full(S_LAT, NLAYERS):
    C = Ctx()
    TT = CTX + S_LAT
    NT, NCT = TT // 128, CTX // 128
    LB = rec_layout(S_LAT)[0]
    NE, NO = (NLAYERS + 1) // 2, max(NLAYERS // 2, 1)
    E = {}
    for name, shape in [("x_in", [TT, D]), ("c2", [2, D]), ("w_mod", [NLAYERS, D, 6 * D]), ("b_mod", [NLAYERS, 6 * D]),
                        ("norm_mix", [NLAYERS, D]), ("norm_ffn", [NLAYERS, D]), ("w_out", [NLAYERS, D, D]),
                        ("wall_e", [NE, D, WALL_EVEN]), ("wall_o", [NO, D, 1536]), ("cosT", [128, TT]), ("sinT", [128, TT]),
                        ("masks", [4, 128, 128]), ("sink", [NE, 8]), ("lamv", [NE, 4, 64]), ("subln", [NE, 128]),
                        ("lconst", [NE, 2]), ("pw", [NO, 2, 4, 128, 64]), ("psc", [NO, 2, 64, 4]), ("cw", [NO, 2, 256, 5]),
                        ("wbd", [NO, 2, 2, 2, 2, 128, 128]), ("gb", [NO, 2, 256, 6]), ("rcnt", [4, LB]),
                        ("wq", [NLAYERS, D, 2048]), ("skT", [NLAYERS, 128, 2048]), ("UT", [NLAYERS, D, NEXP]),
                        ("V", [NLAYERS, NEXP, D]), ("fnw", [D])]:
        E[name] = C.ext_in(name, shape)
    out = C.ext_out("out", [TT, D])
    sc = {}
    for name, shape, dt in [("xbuf", [TT, D], F32), ("x1", [TT, D], F32), ("qT", [512, TT], BF16), ("dqT", [512, TT], BF16),
                            ("dkT", [512, TT], BF16), ("kwT", [256, TT + 256], BF16), ("vw", [TT + 256, 128], BF16),
                            ("dv", [TT, 512], BF16), ("concat", [TT, D], F32), ("pT", [1536, TT], F32),
                            ("concatT", [D, TT], F32), ("h2T", [D, TT], BF16), ("lists", [384, TT], BF16),
                            ("UTb", [D, NEXP], BF16), ("Vb", [NEXP, D], BF16)]:
        sc[name] = C.scratch("sc_" + name, shape, dt)
    for l in range(NLAYERS):
        j = l // 2
        xsrc = E["x_in"] if l == 0 else sc["xbuf"]
        if l % 2 == 0:
            C.begin(dict(xin=xsrc, c2=E["c2"], wmod=E["w_mod"][l][:, 0:2048], bmod=E["b_mod"][l][0:2048], nw=E["norm_mix"][l],
                         wall=E["wall_e"][j], cosT=E["cosT"], sinT=E["sinT"], qT=sc["qT"], dqT=sc["dqT"], dkT=sc["dkT"],
                         kwT=sc["kwT"], vw=sc["vw"], dv=sc["dv"]))
            build_f_even(C, NT, NCT)
            C.end()
            C.begin(dict(qT=sc["qT"], dqT=sc["dqT"], kwT=sc["kwT"], vw=sc["vw"], dkT=sc["dkT"], dv=sc["dv"],
                         masks=E["masks"], sink=E["sink"][j], lamv=E["lamv"][j], subln=E["subln"][j],
                         lconst=E["lconst"][j], concat=sc["concat"]))
            build_att(C, NT, NCT, NCT, NT)
            C.end()
            cname, fm = "concat", False
        else:
            C.begin(dict(xin=xsrc, c2=E["c2"], wmod=E["w_mod"][l][:, 0:2048], bmod=E["b_mod"][l][0:2048], nw=E["norm_mix"][l],
                         wall=E["wall_o"][j], pT=sc["pT"]))
            build_f_odd(C, NT, NCT)
            C.end()
            for half in range(2):
                C.begin(dict(xpT=sc["pT"][0:512], xrT=sc["pT"][512 + half * 256:512 + (half + 1) * 256],
                             gT=sc["pT"][1024 + half * 256:1024 + (half + 1) * 256], pw=E["pw"][j][half], psc=E["psc"][j][half],
                             rcnt=E["rcnt"], cw=E["cw"][j][half], wbd=E["wbd"][j][half], gb=E["gb"][j][half],
                             concatT=sc["concatT"]))
                build_rec(C, S_LAT, half)
                C.end()
            cname, fm = "concatT", True
        C.begin(dict(xin=xsrc, concat=sc[cname], c2=E["c2"], wmod=E["w_mod"][l][:, 2048:5120], bmod=E["b_mod"][l][2048:5120],
                     nw=E["norm_ffn"][l], wout=E["w_out"][l], wq=E["wq"][l], skT=E["skT"][l], x1=sc["x1"], h2T=sc["h2T"],
                     lists=sc["lists"]))
        build_post_a(C, NT, NCT, fm)
        C.end()
        final = (l == NLAYERS - 1)
        C.begin(dict(x1=sc["x1"], h2T=sc["h2T"], lists=sc["lists"], UT=E["UT"][l], V=E["V"][l], UTb=sc["UTb"], Vb=sc["Vb"],
                     c2=E["c2"], wmod=E["w_mod"][l][:, 5120:6144], bmod=E["b_mod"][l][5120:6144], fnw=E["fnw"],
                     xout=out if final else sc["xbuf"]))
        build_post_b(C, NT, NCT, final)
        C.end()
    return C.nc


def _swap_halves_cols(w):
    n = w.shape[1] // 64
    w4 = w.reshape(w.shape[0], n, 2, 32)
    return np.ascontiguousarray(w4[:, :, ::-1, :]).reshape(w.shape[0], n * 64)


def _prep_wall_even(w_in):
    q = w_in[:, 0:512]; k = w_in[:, 512:640]; v = w_in[:, 640:768]
    dq = w_in[:, 768:1280]; dk = w_in[:, 1280:1792]; dv = w_in[:, 1792:2304]
    kd = np.concatenate([k[:, 0:64], k[:, 0:64], k[:, 64:128], k[:, 64:128]], axis=1)
    fm = np.concatenate([q, kd, dq, dk], axis=1)
    return np.ascontiguousarray(np.concatenate([fm, _swap_halves_cols(fm), v, dv], axis=1), dtype=np.float32)


def _rope_tables(tok_pos):
    inv = (ROPE_THETA ** (-np.arange(16, dtype=np.float32) / 16)).astype(np.float32)
    pos = np.maximum(tok_pos, 0)
    row = (pos // GRID_W).astype(np.float32)
    col = (pos % GRID_W).astype(np.float32)
    ang = np.concatenate([row[:, None] * inv, col[:, None] * inv], axis=-1).astype(np.float32)
    c = np.cos(ang).astype(np.float32)
    s = np.sin(ang).astype(np.float32)
    isctx = tok_pos < 0
    c[isctx] = 1.0
    s[isctx] = 0.0
    f = np.arange(128) % 32
    sign = np.where((np.arange(128) % 64) < 32, -1.0, 1.0).astype(np.float32)
    return np.ascontiguousarray(c[:, f].T), np.ascontiguousarray((s[:, f] * sign[None, :]).T)


def _rec_rcnt(NL):
    LB = rec_layout(NL)[0]
    o = np.ones((4, LB), np.float32)
    for g, w in enumerate((2, 4, 8, 16)):
        for (off, Tn) in ((16, 256), (304, NL)):
            t = np.arange(Tn)
            lo = np.clip(t - w // 2, 0, Tn)
            hi = np.clip(t - w // 2 + w, 0, Tn)
            o[g, off:off + Tn] = 1.0 / (hi - lo).astype(np.float32)
    return o


def _rec_core_inputs(half, pool_w, pool_scale, conv_w, conv_b, wa, ba, wx, bx, lam):
    cs = slice(half * 256, (half + 1) * 256)
    pw = np.ascontiguousarray(pool_w[:, :, half * 64:(half + 1) * 64])
    psc = np.ascontiguousarray(pool_scale.reshape(4, 128)[:, half * 64:(half + 1) * 64].T)
    cw = np.ascontiguousarray(np.concatenate([conv_w[:, cs].T, conv_b[cs][:, None]], 1))
    wbd = np.zeros((2, 2, 2, 128, 128), np.float32)
    for gi, W in enumerate((wa, wx)):
        for d in range(2):
            for c in range(2):
                for bb in range(2):
                    blk = half * 4 + c * 2 + bb
                    wbd[gi, d, c, bb * 64:(bb + 1) * 64, bb * 64:(bb + 1) * 64] = W[d, blk]
    gb = np.ascontiguousarray(np.stack([ba[0, cs], ba[1, cs], bx[0, cs], bx[1, cs], lam[0, cs], lam[1, cs]], 1))
    return pw, psc, cw, wbd, gb


def _lambda_init(layer):
    return 0.8 - 0.6 * math.exp(-0.3 * layer)


def host_inputs(x, c, ctx, c_ctx, w_mod, b_mod, norm_mix, norm_ffn, w_out, attn_w_in, swa_sink,
                diff_lambda, diff_subln, rec_w_in, pool_w, pool_scale, lru_conv_w, lru_conv_b,
                lru_wa, lru_ba, lru_wx, lru_bx, lru_lambda, peer_wq, peer_subkeys, peer_u, peer_v,
                final_norm, NLAYERS=4):
    f32 = np.float32
    A = lambda a: np.ascontiguousarray(np.asarray(a), dtype=f32)
    x, c, ctx, c_ctx = A(x), A(c), A(ctx), A(c_ctx)
    B, S_LAT = x.shape[0], x.shape[1]
    NE, NO = (NLAYERS + 1) // 2, max(NLAYERS // 2, 1)
    pos = np.concatenate([-np.ones(CTX, np.int64), np.arange(S_LAT)])
    cosT, sinT = _rope_tables(pos)
    ii_ = np.arange(128)[:, None]
    jj_ = np.arange(128)[None, :]
    mP = (jj_ <= ii_).astype(f32)
    mN = (ii_ <= jj_).astype(f32)
    Z = np.zeros((128, 128), f32)
    sh = dict(
        w_mod=A(w_mod)[:NLAYERS], b_mod=A(b_mod)[:NLAYERS], norm_mix=A(norm_mix)[:NLAYERS], norm_ffn=A(norm_ffn)[:NLAYERS],
        w_out=A(w_out)[:NLAYERS],
        wall_e=np.stack([_prep_wall_even(A(attn_w_in[j])) for j in range(NE)], 0),
        wall_o=A(rec_w_in)[:NO], cosT=cosT, sinT=sinT, masks=np.stack([Z, mP, mN, Z], 0),
        sink=A(swa_sink)[:NE], lamv=A(diff_lambda)[:NE], subln=A(diff_subln)[:NE],
        lconst=np.array([[_lambda_init(2 * j), 1.0 - _lambda_init(2 * j)] for j in range(NE)], f32),
        rcnt=_rec_rcnt(S_LAT), wq=A(peer_wq)[:NLAYERS],
        skT=np.stack([np.ascontiguousarray(A(peer_subkeys[l]).reshape(16, 128, 128).transpose(2, 0, 1).reshape(128, 2048))
                      for l in range(NLAYERS)], 0),
        UT=np.stack([np.ascontiguousarray(A(peer_u[l]).T) for l in range(NLAYERS)], 0),
        V=A(peer_v)[:NLAYERS], fnw=A(final_norm))
    recs = [[_rec_core_inputs(h, A(pool_w[j]), A(pool_scale[j]), A(lru_conv_w[j]), A(lru_conv_b[j]), A(lru_wa[j]),
                              A(lru_ba[j]), A(lru_wx[j]), A(lru_bx[j]), A(lru_lambda[j])) for h in range(2)] for j in range(NO)]
    for k_, nm in enumerate(("pw", "psc", "cw", "wbd", "gb")):
        sh[nm] = np.stack([np.stack([recs[j][h][k_] for h in range(2)], 0) for j in range(NO)], 0)
    ims = []
    for i in range(min(NCORES, B)):
        b = i % B
        d = dict(sh)
        d["x_in"] = np.concatenate([ctx[b], x[b]], 0)
        d["c2"] = np.stack([c[b], c_ctx], 0)
        ims.append(d)
    return ims, B, S_LAT


def kernel(**inputs):
    ims, B, S_LAT = host_inputs(**inputs)
    key = (S_LAT, 4)
    if key not in _PROG_CACHE:
        _PROG_CACHE[key] = build_full(S_LAT, 4)
    res = run_bass_kernel_spmd(_PROG_CACHE[key], ims, core_ids=list(range(len(ims))))
    out = np.stack([np.asarray(res.results[b]["out"])[CTX:] for b in range(B)], 0)
    return out.astype(np.float32)
```
